# Optimizing a Trainium2 kernel written in Bass

```python
import jax, jax.numpy as jnp
from jax import lax
import numpy as np

D_MODEL = 1024
BATCH = 4
SEQ = 4096
DEPTH = 2

GRID_W = 64
D_GROUP = 256
N_GROUPS = 4
D_MIX = N_GROUPS * D_GROUP
HEAD_DIM = 64
CONV_A_WIDTH = 31
CONV_B_WIDTH = 3
SWA_Q_HEADS = D_GROUP // HEAD_DIM
SWA_KV_HEADS = 2
SWA_WINDOW = 128
SWA_BLOCK = 128
NA_HEADS = D_GROUP // HEAD_DIM
NA_KH_MAX = 8
NA_KW = 16
ROPE_THETA = 10000.0
EPS = 1e-6
NEG_INF = -1e30

SPLIT_SIZES = (
    D_GROUP, D_GROUP, D_GROUP,
    D_GROUP, D_GROUP, D_GROUP, D_GROUP,
    SWA_Q_HEADS * HEAD_DIM, SWA_KV_HEADS * HEAD_DIM,
    SWA_KV_HEADS * HEAD_DIM, D_GROUP,
    NA_HEADS * HEAD_DIM, NA_HEADS * HEAD_DIM,
    NA_HEADS * HEAD_DIM, D_GROUP,
)
D_IN = sum(SPLIT_SIZES)

kernel_name = "hymba_style_bidir_hybrid_encoder"


def rmsnorm(x, g):
    x32 = x.astype(jnp.float32)
    y = x32 * lax.rsqrt(jnp.mean(x32 * x32, axis=-1, keepdims=True) + EPS)
    return y.astype(x.dtype) * g


def layernorm(x, g, b):
    x32 = x.astype(jnp.float32)
    mu = jnp.mean(x32, axis=-1, keepdims=True)
    xc = x32 - mu
    y = xc * lax.rsqrt(jnp.mean(xc * xc, axis=-1, keepdims=True) + EPS)
    return y.astype(x.dtype) * g + b


def depthwise_conv(x, w):
    width, ch = w.shape
    pad = (width - 1) // 2
    return lax.conv_general_dilated(
        x, w[:, None, :], window_strides=(1,), padding=[(pad, pad)],
        dimension_numbers=("NWC", "WIO", "NWC"), feature_group_count=ch)


def rope(x, pos):
    d = x.shape[-1]
    inv_freq = ROPE_THETA ** (-jnp.arange(0, d, 2, dtype=jnp.float32) / d)
    ang = pos[:, None] * inv_freq[None, :]
    cos = jnp.cos(ang)[None, :, None, :].astype(x.dtype)
    sin = jnp.sin(ang)[None, :, None, :].astype(x.dtype)
    x1, x2 = x[..., : d // 2], x[..., d // 2:]
    return jnp.concatenate([x1 * cos - x2 * sin, x2 * cos + x1 * sin], axis=-1)


def conformer_conv(u, v, conv_w, conv_b, ln_g, ln_b):
    h = u * jax.nn.sigmoid(v)
    h = depthwise_conv(h, conv_w) + conv_b
    h = layernorm(h, ln_g, ln_b)
    return jax.nn.silu(h)


def short_gated_conv(bg, cg, xv, conv_w):
    return bg * depthwise_conv(cg * xv, conv_w)


def window_gqa(q, k, v, sink):
    b, s, hq, d = q.shape
    hkv = k.shape[2]
    g = hq // hkv
    nb = s // SWA_BLOCK
    pos = jnp.arange(s, dtype=jnp.float32)
    q = rope(q, pos)
    k = rope(k, pos)
    qb = q.reshape(b, nb, SWA_BLOCK, hkv, g, d)
    pad = ((0, 0), (SWA_BLOCK, SWA_BLOCK), (0, 0), (0, 0))
    kp = jnp.pad(k, pad).reshape(b, nb + 2, SWA_BLOCK, hkv, d)
    vp = jnp.pad(v, pad).reshape(b, nb + 2, SWA_BLOCK, hkv, d)
    kw = jnp.concatenate([kp[:, :-2], kp[:, 1:-1], kp[:, 2:]], axis=2)
    vw = jnp.concatenate([vp[:, :-2], vp[:, 1:-1], vp[:, 2:]], axis=2)
    scores = jnp.einsum("bnqhgd,bnkhd->bnhgqk", qb, kw).astype(jnp.float32) * (d ** -0.5)
    blk = jnp.arange(nb)[:, None, None]
    qpos = blk * SWA_BLOCK + jnp.arange(SWA_BLOCK)[None, :, None]
    kpos = (blk - 1) * SWA_BLOCK + jnp.arange(3 * SWA_BLOCK)[None, None, :]
    valid = (jnp.abs(kpos - qpos) <= SWA_WINDOW) & (kpos >= 0) & (kpos < s)
    scores = jnp.where(valid[None, :, None, None], scores, NEG_INF)
    sink_col = jnp.broadcast_to(sink.astype(jnp.float32).reshape(1, 1, hkv, g, 1, 1),
                                scores.shape[:-1] + (1,))
    p = jax.nn.softmax(jnp.concatenate([scores, sink_col], axis=-1), axis=-1)[..., :-1]
    o = jnp.einsum("bnhgqk,bnkhd->bnqhgd", p.astype(v.dtype), vw)
    return o.reshape(b, s, hq * d)


def neighborhood_attn(q, k, v, rpb):
    b, s, h, d = q.shape
    rows = s // GRID_W
    kh = min(NA_KH_MAX, rows)
    r = jnp.arange(rows)
    r0 = jnp.clip(r - kh // 2, 0, rows - kh)
    row_idx = r0[:, None] + jnp.arange(kh)[None, :]
    qg = q.reshape(b, rows, GRID_W, h, d)
    kg = jnp.take(k.reshape(b, rows, GRID_W, h, d), row_idx, axis=1)
    vg = jnp.take(v.reshape(b, rows, GRID_W, h, d), row_idx, axis=1)
    scores = jnp.einsum("brqhd,brikhd->brhqik", qg, kg).astype(jnp.float32) * (d ** -0.5)
    c = jnp.arange(GRID_W)
    c0 = jnp.clip(c - NA_KW // 2, 0, GRID_W - NA_KW)
    col_ok = (c[None, :] >= c0[:, None]) & (c[None, :] < c0[:, None] + NA_KW)
    dr = row_idx - r[:, None] + (NA_KH_MAX - 1)
    dc = jnp.clip(c[None, :] - c[:, None], -(NA_KW - 1), NA_KW - 1) + (NA_KW - 1)
    bias = jnp.take(rpb[:, dr], dc, axis=-1)
    bias = bias.transpose(1, 0, 3, 2, 4).astype(jnp.float32)
    scores = jnp.where(col_ok[:, None, :], scores + bias[None], NEG_INF)
    p = jax.nn.softmax(scores.reshape(b, rows, h, GRID_W, kh * GRID_W), axis=-1)
    p = p.reshape(scores.shape).astype(v.dtype)
    o = jnp.einsum("brhqik,brikhd->brqhd", p, vg)
    return o.reshape(b, s, h * d)


def hybrid_layer(x, norm_g, w_in, w_out, conv_a_w, conv_a_b, ln_a_g, ln_a_b,
                 conv_b_w, swa_sink, na_rpb):
    b, s, _ = x.shape
    h = rmsnorm(x, norm_g)
    proj = jnp.einsum("bsd,de->bse", h, w_in)
    split_points = np.cumsum(SPLIT_SIZES)[:-1].tolist()
    (a_u, a_v, a_z,
     b_b, b_c, b_x, b_z,
     c_q, c_k, c_v, c_z,
     d_q, d_k, d_v, d_z) = jnp.split(proj, split_points, axis=-1)
    y_a = conformer_conv(a_u, a_v, conv_a_w, conv_a_b, ln_a_g, ln_a_b) * jax.nn.silu(a_z)
    y_b = short_gated_conv(b_b, b_c, b_x, conv_b_w) * jax.nn.silu(b_z)
    y_c = window_gqa(c_q.reshape(b, s, SWA_Q_HEADS, HEAD_DIM),
                     c_k.reshape(b, s, SWA_KV_HEADS, HEAD_DIM),
                     c_v.reshape(b, s, SWA_KV_HEADS, HEAD_DIM), swa_sink) * jax.nn.silu(c_z)
    y_d = neighborhood_attn(d_q.reshape(b, s, NA_HEADS, HEAD_DIM),
                            d_k.reshape(b, s, NA_HEADS, HEAD_DIM),
                            d_v.reshape(b, s, NA_HEADS, HEAD_DIM), na_rpb) * jax.nn.silu(d_z)
    y = jnp.concatenate([y_a, y_b, y_c, y_d], axis=-1)
    return x + jnp.einsum("bse,ed->bsd", y, w_out)


def setup_inputs(seed: int = 0) -> dict:
    key = jax.random.key(seed)
    ks = jax.random.split(key, 12)
    f32 = jnp.float32
    x = jax.random.normal(ks[0], (BATCH, SEQ, D_MODEL), f32)
    norm_g = 1.0 + 0.05 * jax.random.normal(ks[1], (DEPTH, D_MODEL), f32)
    w_in = jax.random.normal(ks[2], (DEPTH, D_MODEL, D_IN), f32) * D_MODEL ** -0.5
    w_out = jax.random.normal(ks[3], (DEPTH, D_MIX, D_MODEL), f32) * D_MIX ** -0.5
    conv_a_w = jax.random.normal(ks[4], (DEPTH, CONV_A_WIDTH, D_GROUP), f32) * CONV_A_WIDTH ** -0.5
    conv_a_b = 0.02 * jax.random.normal(ks[5], (DEPTH, D_GROUP), f32)
    ln_a_g = 1.0 + 0.05 * jax.random.normal(ks[6], (DEPTH, D_GROUP), f32)
    ln_a_b = 0.02 * jax.random.normal(ks[7], (DEPTH, D_GROUP), f32)
    conv_b_w = jax.random.normal(ks[8], (DEPTH, CONV_B_WIDTH, D_GROUP), f32) * CONV_B_WIDTH ** -0.5
    swa_sink = jax.random.normal(ks[9], (DEPTH, SWA_Q_HEADS), f32)
    na_rpb = 0.1 * jax.random.normal(ks[10], (DEPTH, NA_HEADS, 2 * NA_KH_MAX - 1, 2 * NA_KW - 1), f32)
    final_norm_g = 1.0 + 0.05 * jax.random.normal(ks[11], (D_MODEL,), f32)
    return {"x": x, "norm_g": norm_g, "w_in": w_in, "w_out": w_out,
            "conv_a_w": conv_a_w, "conv_a_b": conv_a_b, "ln_a_g": ln_a_g, "ln_a_b": ln_a_b,
            "conv_b_w": conv_b_w, "swa_sink": swa_sink, "na_rpb": na_rpb,
            "final_norm_g": final_norm_g}


def reference(x, norm_g, w_in, w_out, conv_a_w, conv_a_b, ln_a_g, ln_a_b,
              conv_b_w, swa_sink, na_rpb, final_norm_g):
    for l in range(DEPTH):
        x = hybrid_layer(x, norm_g[l], w_in[l], w_out[l], conv_a_w[l], conv_a_b[l],
                         ln_a_g[l], ln_a_b[l], conv_b_w[l], swa_sink[l], na_rpb[l])
    return rmsnorm(x, final_norm_g)
```

```python
import numpy as np
import concourse.bass as bass
import concourse.mybir as mybir
from contextlib import ExitStack
from concourse.bass_utils import run_bass_kernel_spmd

F32 = mybir.dt.float32
BF16 = mybir.dt.bfloat16
ALU = mybir.AluOpType
AF = mybir.ActivationFunctionType


STRICT_SAME_ENGINE = True


class _Op:
    __slots__ = ("eng", "fn", "deps", "is_dma", "needs_inc", "tok")

    def __init__(self, eng, fn, is_dma):
        self.eng = eng
        self.fn = fn
        self.deps = []
        self.is_dma = is_dma
        self.needs_inc = is_dma
        self.tok = None


class _St:
    __slots__ = ("w", "rd")

    def __init__(self):
        self.w = None
        self.rd = []


class Prog:
    ENGS = ("pe", "act", "dve", "pool", "sp")
    NDMA = {"sp": 8, "pool": 4, "act": 2}

    def __init__(self, nc):
        self.nc = nc
        self.ops = {e: [] for e in self.ENGS}
        self.state = {}
        self.dma_last = {q: [None] * n for q, n in self.NDMA.items()}
        self.dma_cnt = {q: 0 for q in self.NDMA}
        self.dma_ops = []
        self.cur_fence = []
        self.phase = 0
        self.last_ws = {}
        self.dma_last_ws = {q: [None] * n for q, n in self.NDMA.items()}

    def fence(self):
        f = [o for o in self.last_ws.values()]
        for q in self.NDMA:
            for o in self.dma_last_ws[q]:
                if o is not None:
                    f.append(o)
        self.cur_fence = f
        self.phase += 1

    def soft_fence(self):
        ph = self.phase
        self.fence()
        self.phase = ph

    def K(self, *a):
        return ("ws", self.phase) + a

    def _get(self, k):
        st = self.state.get(k)
        if st is None:
            st = self.state[k] = _St()
            if isinstance(k, tuple) and k and k[0] == "ws":
                st.rd = list(self.cur_fence)
        return st

    def _add_dep(self, o, p, raw):
        if p is None or p is o:
            return
        if (not o.is_dma) and (not p.is_dma) and p.eng == o.eng:
            if o.eng == "pe" or (not raw and not STRICT_SAME_ENGINE):
                return
        o.deps.append(p)

    def op(self, eng, fn, reads=(), writes=(), dma=False):
        o = _Op(eng, fn, dma)
        for k in reads:
            st = self._get(k)
            self._add_dep(o, st.w, True)
        for k in writes:
            st = self._get(k)
            self._add_dep(o, st.w, False)
            for r in st.rd:
                self._add_dep(o, r, False)
        if dma:
            q = eng
            slot = self.dma_cnt[q] % self.NDMA[q]
            self.dma_cnt[q] += 1
            prev = self.dma_last[q][slot]
            if prev is not None:
                o.deps.append(prev)
            self.dma_last[q][slot] = o
            o.tok = (q, slot)
            self.dma_ops.append(o)
        for k in reads:
            st = self.state[k]
            if not dma:
                st.rd = [r for r in st.rd if r.is_dma or r.eng != eng]
            st.rd.append(o)
        for k in writes:
            st = self.state[k]
            st.w = o
            st.rd = []
        for p in o.deps:
            p.needs_inc = True
        self.ops[eng].append(o)
        if any(isinstance(k, tuple) and k and k[0] == "ws" for k in list(reads) + list(writes)):
            if dma:
                self.dma_last_ws[eng][slot] = o
            else:
                self.last_ws[eng] = o
        return o

    def emit(self, final_wait_ops=()):
        nc = self.nc
        sems = {e: nc.alloc_semaphore("s_" + e) for e in ("pe", "act", "dve", "pool")}
        dsems = {q: [nc.alloc_semaphore("d_%s%d" % (q, i)) for i in range(n)] for q, n in self.NDMA.items()}
        for e in self.ENGS:
            cnt = 0
            dcnt = {}
            for o in self.ops[e]:
                if o.is_dma:
                    q, slot = o.tok
                    dcnt[(q, slot)] = dcnt.get((q, slot), 0) + 16
                    o.tok = (dsems[q][slot], dcnt[(q, slot)])
                elif o.needs_inc:
                    cnt += 1
                    o.tok = (sems[e], cnt)
        self.stats = {}
        with nc.Block() as block:
            def mk(e):
                def body(eng):
                    waited = {}
                    nw = 0
                    for o in self.ops[e]:
                        for p in o.deps:
                            s, v = p.tok
                            if waited.get(s.num, 0) < v:
                                eng.wait_ge(s, v)
                                waited[s.num] = v
                                nw += 1
                        ins = o.fn(eng)
                        if o.is_dma:
                            ins.then_inc(o.tok[0], 16)
                        elif o.needs_inc:
                            ins.then_inc(o.tok[0], 1)
                    if e == "sp":
                        for p in final_wait_ops:
                            s, v = p.tok
                            if waited.get(s.num, 0) < v:
                                eng.wait_ge(s, v)
                                waited[s.num] = v
                    self.stats[e] = (len(self.ops[e]), nw)
                return body
            block.tensor(mk("pe"))
            block.scalar(mk("act"))
            block.vector(mk("dve"))
            block.gpsimd(mk("pool"))
            block.sync(mk("sp"))


D = 1024
TK = 128
NT_MAX = 20
NK = (20, 18)
NQ = (18, 16)
NOUT = 16
EPS = 1e-6
PADA = 16
PADB = 2

_O = dict(a_u=0, a_v=256, a_z=512, b_b=768, b_c=1024, b_x=1280, b_z=1536,
          c_q=1792, c_k=2048, c_v=2176, c_z=2304, d_q=2560, d_k=2816, d_v=3072, d_z=3328)


def _swap64(cols):
    cols = np.asarray(cols).reshape(-1, 64)
    return np.concatenate([cols[:, 32:], cols[:, :32]], axis=1).reshape(-1)


def _build_units():
    r = np.arange
    cq = _O["c_q"]
    cz = _O["c_z"]
    hq = lambda h: cq + h * 64 + r(64)
    hz = lambda h: cz + h * 64 + r(64)
    q0 = np.concatenate([hq(0), hq(2)])
    q1 = np.concatenate([hq(1), hq(3)])
    z0 = np.concatenate([hz(0), hz(2)])
    z1 = np.concatenate([hz(1), hz(3)])
    ck = _O["c_k"] + r(128)
    units = [
        ("A_key", np.concatenate([_O["a_u"] + r(256), _O["a_v"] + r(256)])),
        ("A_q", _O["a_z"] + r(256)),
        ("B_key", np.concatenate([_O["b_c"] + r(256), _O["b_x"] + r(256)])),
        ("B_q", np.concatenate([_O["b_b"] + r(256), _O["b_z"] + r(256)])),
        ("C_key", np.concatenate([ck, _O["c_v"] + r(128)])),
        ("C_z", np.concatenate([z0, z1])),
        ("C_q", np.concatenate([q0, q1])),
        ("D_key", np.concatenate([_O["d_k"] + r(256), _O["d_v"] + r(256)])),
        ("D_q", np.concatenate([_O["d_q"] + r(256), _O["d_z"] + r(256)])),
    ]
    return units


UNITS = _build_units()
UNIT_OFF = {}
_off = 0
for _n, _c in UNITS:
    UNIT_OFF[_n] = (_off, len(_c) // 128)
    _off += len(_c)
NCOLS = _off
COLPERM = np.concatenate([c for _, c in UNITS])
_c0 = 512
ROWPERM = np.concatenate([np.arange(0, 512),
                          _c0 + 0 * 64 + np.arange(64), _c0 + 2 * 64 + np.arange(64),
                          _c0 + 1 * 64 + np.arange(64), _c0 + 3 * 64 + np.arange(64),
                          np.arange(768, 1024)])
SINK_ORDER = [0, 2, 1, 3]
NA_CLASS_CH = (4, 4, 5)
NA_CLASS_OFF = (0, 4, 8)
NA_NCH = 13


def _sbs(ntiles):
    out = []
    t = 0
    while t < ntiles:
        n = min(4, ntiles - t)
        out.append((t * TK, n * TK))
        t += n
    return out


def build_program(layers=(0, 1), do_final=True, x_in_tiles=20, out_tiles=16):
    nc = bass.Bass("TRN2", target_bir_lowering=False)
    P = Prog(nc)
    dt_in = lambda name, shape, dt=F32: nc.dram_tensor(name, shape, dt, kind="ExternalInput").ap()

    x_d = dt_in("x", [x_in_tiles * TK, D])
    rc_d = dt_in("rope_cos", [128, NT_MAX * TK])
    rs_d = dt_in("rope_sin", [128, NT_MAX * TK])
    fg_d = dt_in("fg", [1, D])
    psw_d = dt_in("psw", [128, 128])
    L = {}
    for l in layers:
        L[l] = dict(
            win=dt_in("win%d" % l, [D, NCOLS]), wout=dt_in("wout%d" % l, [D, D]), ng=dt_in("ng%d" % l, [1, D]),
            caw=dt_in("caw%d" % l, [128, 2, 31]), cab=dt_in("cab%d" % l, [128, 2]), lng=dt_in("lng%d" % l, [128, 2]),
            lnb=dt_in("lnb%d" % l, [128, 2]), cbw=dt_in("cbw%d" % l, [128, 2, 3]), sink=dt_in("sink%d" % l, [1, 4]),
            nab=dt_in("nab%d" % l, [128, NA_NCH, 4, 128]))
    out_d = nc.dram_tensor("out", [out_tiles * TK, D], F32, kind="ExternalOutput").ap()

    A = nc.alloc_sbuf_tensor
    ws = {"es": None}

    def phase_begin():
        if ws["es"] is not None:
            ws["es"].close()
        ws["es"] = ExitStack()
        P.fence()

    def WS(name, shape, dt):
        return ws["es"].enter_context(nc.sbuf_tensor("sb_" + name, shape, dt))
    x_res = A("x_res", [128, NT_MAX, D], F32)
    hT = A("hT", [128, 8, NT_MAX * TK], BF16)
    yT = A("yT", [128, 4, NQ[0] * TK], BF16)
    wring = [A("wring%d" % i, [128, 8, 512], BF16) for i in range(2)]
    wo = A("wo", [128, 4, D], BF16)
    ident = A("ident", [128, 128], BF16)
    identf = A("identf", [128, 128], F32)
    ones256 = A("ones256", [128, 128], BF16)
    mask3 = A("mask3", [128, 3, 128], BF16)
    ssq = A("ssq", [128, NT_MAX], F32)
    psw = A("psw_sb", [128, 128], BF16)
    rstd = A("rstd", [128, NT_MAX], F32)
    s0 = nc.alloc_psum_tensor("ps_s0", [128, 1024], F32)
    s64 = nc.alloc_psum_tensor("ps_s64", [128, 1024], F32)
    gen_t = [nc.alloc_psum_tensor("ps_g%d" % i, [128, 512], F32) for i in range(4)]
    gen = [gen_t[0][:, :], gen_t[1][:, :], gen_t[2][:, :], gen_t[3][:, :],
           s0[:, 0:512], s0[:, 512:1024], s64[:, 0:512], s64[:, 512:1024]]
    BKEYS = [("g", 0), ("g", 1), ("g", 2), ("g", 3), ("s0", 0), ("s0", 1), ("s64", 0), ("s64", 1)]
    gcnt = [0]
    gmode = {"wide": True}
    WIDE = [0, 1, 2, 4, 5, 6, 7]
    NARROW = [0]
    PV_BANKS = [1, 2]
    pvc = [0]

    def gbank():
        pool = WIDE if gmode["wide"] else NARROW
        i = pool[gcnt[0] % len(pool)]
        gcnt[0] += 1
        return i

    def bkey(b):
        return BKEYS[b]

    TB_BANK = 3

    for i in range(x_in_tiles):
        P.op("sp", lambda e, i=i: e.dma_start(out=x_res[:, i, :], in_=x_d[i * TK:(i + 1) * TK, :]),
             writes=[("x", i)], dma=True)

    P.op("pool", lambda e: e.dma_start(out=psw[:, :], in_=psw_d), writes=["psw"], dma=True)
    P.op("pool", lambda e: e.memset(identf[:, :], 1.0), writes=["identf"])
    P.op("pool", lambda e: e.affine_select(out=identf[:, :], in_=identf[:, :], pattern=[[-1, 128]],
                                           compare_op=ALU.is_equal, fill=0.0, base=0, channel_multiplier=1),
         reads=["identf"], writes=["identf"])
    P.op("pool", lambda e: e.tensor_copy(out=ident[:, :], in_=identf[:, :]), reads=["identf"], writes=["ident"])
    P.op("pool", lambda e: e.memset(ones256[:, :], 1.0 / 256.0), writes=["ones256"])
    P.op("pool", lambda e: e.memset(mask3[:, :, :], 1.0), writes=["mask3"])
    P.op("pool", lambda e: e.affine_select(out=mask3[:, 0, :], in_=mask3[:, 0, :], pattern=[[-1, 128]],
                                           compare_op=ALU.is_ge, fill=0.0, base=0, channel_multiplier=1),
         reads=["mask3"], writes=["mask3"])
    P.op("pool", lambda e: e.affine_select(out=mask3[:, 2, :], in_=mask3[:, 2, :], pattern=[[1, 128]],
                                           compare_op=ALU.is_ge, fill=0.0, base=0, channel_multiplier=-1),
         reads=["mask3"], writes=["mask3"])

    wstate = {"n": 0}

    def load_unit(l, name):
        slot = wstate["n"] % 2
        wstate["n"] += 1
        off, nb = UNIT_OFF[name]
        src = L[l]["win"][:, off:off + nb * 128].rearrange("(c p) n -> p c n", p=128)
        P.op("pool", lambda e: e.dma_start(out=wring[slot][:, :, 0:nb * 128], in_=src),
             writes=[("w", slot)], dma=True)
        return slot

    def load_wo(l, pair):
        src = L[l]["wout"][pair * 512:(pair + 1) * 512, :].rearrange("(c p) n -> p c n", p=128)
        P.op("pool", lambda e: e.dma_start(out=wo[:, :, :], in_=src), writes=["wo"], dma=True)

    def hkeys(t0, n):
        return [("hT", t) for t in range(t0 // TK, (t0 + n) // TK)]

    def proj_fm(slot, blk, t0, n, bank):
        for c in range(8):
            P.op("pe", lambda e, c=c: e.matmul(gen[bank][:, 0:n], lhsT=wring[slot][:, c, blk * 128:(blk + 1) * 128],
                                               rhs=hT[:, c, t0:t0 + n], start=(c == 0), stop=(c == 7)),
                 reads=[("w", slot)] + hkeys(t0, n), writes=[bkey(bank)])

    def proj_tm(slot, col0, ncol, tile, bank):
        for c in range(8):
            P.op("pe", lambda e, c=c: e.matmul(gen[bank][:, 0:ncol], lhsT=hT[:, c, tile * TK:(tile + 1) * TK],
                                               rhs=wring[slot][:, c, col0:col0 + ncol], start=(c == 0), stop=(c == 7)),
                 reads=[("w", slot), ("hT", tile)], writes=[bkey(bank)])

    first_layer = layers[0]
    next_slot = {"slot": load_unit(first_layer, "A_key")}
    order = ["A_key", "A_q", "B_key", "B_q", "C_key", "C_z", "C_q", "D_key", "D_q"]

    def advance(l, cur):
        slot = next_slot["slot"]
        i = order.index(cur)
        if i + 1 < len(order):
            next_slot["slot"] = load_unit(l, order[i + 1])
        else:
            li = layers.index(l)
            if li + 1 < len(layers):
                next_slot["slot"] = load_unit(layers[li + 1], "A_key")
        return slot

    def run_pipeline(items, proj_fn, post_fn, post_pe_fn, nsb, warm_fn=None, nwarm=0):
        for st in proj_fn(0):
            st()
        steps = []
        pend = None
        pend_pe = []
        cur_bo = [None]

        def finish(it):
            if it["first"]:
                cur_bo[0] = PV_BANKS[pvc[0] % 2]
                pvc[0] += 1
            it["bo"] = cur_bo[0]
            it["PV"]()
            for _ in range(nwarm):
                warm_fn(it["bo"])
            if it["last"]:
                post_fn(it)
                it["_age"] = 0
                pend_pe.append(it)

        queue = []
        next_proj = [1]
        sb_done = {}
        cur_k = [0]
        for it in items:
            if it["k"] != cur_k[0]:
                cur_k[0] = it["k"]
                while steps:
                    steps.pop(0)()
            if next_proj[0] < nsb and it["k"] >= next_proj[0] - 1 and (next_proj[0] < 2 or sb_done.get(next_proj[0] - 2)):
                steps.extend(proj_fn(next_proj[0]))
                next_proj[0] += 1
            if steps and (it["k"] + 1 < nsb) and it["ti"] >= 1:
                steps.pop(0)()
            it["S"]()
            it["E"]()
            if it.get("filler") is not None:
                it["filler"]()
            while pend_pe and pend_pe[0]["_age"] >= PE_DEFER:
                pit = pend_pe.pop(0)
                post_pe_fn(pit)
                if pit["last_of_sb"]:
                    sb_done[pit["k"]] = True
            for pit in pend_pe:
                pit["_age"] += 1
            queue.append(it)
            if len(queue) > PIPE_DEPTH:
                finish(queue.pop(0))
        while queue:
            finish(queue.pop(0))
            for pit in list(pend_pe):
                post_pe_fn(pit)
            del pend_pe[:]

    def norm_begin(l, fresh_phase, stack=None, nxs=2):
        if fresh_phase:
            phase_begin()
        tag = "f" if l is None else str(l)
        ctx = dict(final=(l is None))
        WSx = WS if stack is None else (lambda name, shape, dt: stack.enter_context(nc.sbuf_tensor("sb_" + name, shape, dt)))
        ctx["xs"] = [WSx("xs%s_%d" % (tag, i), [128, D], BF16) for i in range(nxs)]
        ctx["bank"] = {}
        ctx["junk"] = WSx("junk%s" % tag, [128, D], BF16)
        ctx["g_bc"] = WSx("g_bc%s" % tag, [128, D], F32)
        ctx["kj"], ctx["kg"], ctx["kx"] = P.K("junk"), P.K("g_bc"), [P.K("xs", i) for i in range(nxs)]
        src = fg_d if l is None else L[l]["ng"]
        g_bc = ctx["g_bc"]
        P.op("act" if (l is not None and l == layers[0] and not fresh_phase) else "sp",
             lambda e: e.dma_start(out=g_bc[:, :], in_=src.broadcast_to([128, D])), writes=[ctx["kg"]], dma=True)
        return ctx

    def norm_batch(ctx, tl, stage="all"):
        xs, junk, g_bc = ctx["xs"], ctx["junk"], ctx["g_bc"]
        junkf = ctx["junk"]
        nx = len(xs)
        if stage in ("b", "c"):
            for i in tl:
                s_ = i % nx
                if stage == "b":
                    bk = gbank()
                    pv = gen[bk][:, :].bitcast(BF16)
                    ctx["bank"][i] = (bk, pv)
                    for c in range(8):
                        P.op("pe", lambda e, c=c, s_=s_, pv=pv: e.transpose(out=pv[:, c * 128:(c + 1) * 128],
                                                                            in_=xs[s_][:, c * 128:(c + 1) * 128],
                                                                            identity=ident[:, :]),
                             reads=[ctx["kx"][s_], "ident"], writes=[bkey(bk)])
                else:
                    bk, pv = ctx["bank"][i]
                    if ctx.get("copy_alt") and i % 2 == 1:
                        P.op("dve", lambda e, i=i, pv=pv: e.tensor_copy(out=hT[:, :, i * TK:(i + 1) * TK],
                                                                        in_=pv.rearrange("p (c t) -> p c t", c=8)),
                             reads=[bkey(bk)], writes=[("hT", i)])
                    else:
                        P.op("act", lambda e, i=i, pv=pv: e.activation(out=hT[:, :, i * TK:(i + 1) * TK],
                                                                       in_=pv.rearrange("p (c t) -> p c t", c=8), func=AF.Copy),
                             reads=[bkey(bk)], writes=[("hT", i)])
            return
        for i in tl:
            if SSQ_ON_DVE:
                P.op("dve", lambda e, i=i: e.scalar_tensor_tensor(out=junkf[:, :], in0=x_res[:, i, :], scalar=1.0,
                                                                   in1=x_res[:, i, :], op0=ALU.mult, op1=ALU.mult,
                                                                   accum_out=ssq[:, i:i + 1]),
                     reads=[("x", i)], writes=[ctx["kj"], ("ssq", i)])
            else:
                P.op("act", lambda e, i=i: e.activation(out=junk[:, :], in_=x_res[:, i, :], func=AF.Square,
                                                        accum_out=ssq[:, i:i + 1]),
                     reads=[("x", i)], writes=[ctx["kj"], ("ssq", i)])
        a, b = tl[0], tl[-1] + 1
        P.op("act", lambda e: e.activation(out=rstd[:, a:b], in_=ssq[:, a:b], func=AF.Sqrt, bias=EPS, scale=1.0 / D),
             reads=[("ssq", i) for i in tl], writes=[("rstd", i) for i in tl])
        P.op("dve", lambda e: e.reciprocal(out=rstd[:, a:b], in_=rstd[:, a:b]),
             reads=[("rstd", i) for i in tl], writes=[("rstd", i) for i in tl])
        for i in tl:
            if ctx["final"]:
                P.op("dve", lambda e, i=i: e.scalar_tensor_tensor(out=x_res[:, i, :], in0=x_res[:, i, :], scalar=rstd[:, i:i + 1],
                                                                  in1=g_bc[:, :], op0=ALU.mult, op1=ALU.mult),
                     reads=[("x", i), ("rstd", i), ctx["kg"]], writes=[("x", i)])
                P.op("sp", lambda e, i=i: e.dma_start(out=out_d[i * TK:(i + 1) * TK, :], in_=x_res[:, i, :]),
                     reads=[("x", i)], dma=True)
                continue
            s_ = i % nx
            P.op("dve", lambda e, i=i, s_=s_: e.scalar_tensor_tensor(out=xs[s_][:, :], in0=x_res[:, i, :],
                                                                     scalar=rstd[:, i:i + 1], in1=g_bc[:, :],
                                                                     op0=ALU.mult, op1=ALU.mult),
                 reads=[("x", i), ("rstd", i), ctx["kg"]], writes=[ctx["kx"][s_]])
            if stage == "a":
                continue
            bk = gbank()
            pv = gen[bk][:, :].bitcast(BF16)
            for c in range(8):
                P.op("pe", lambda e, c=c, s_=s_, pv=pv: e.transpose(out=pv[:, c * 128:(c + 1) * 128],
                                                                    in_=xs[s_][:, c * 128:(c + 1) * 128],
                                                                    identity=ident[:, :]),
                     reads=[ctx["kx"][s_], "ident"], writes=[bkey(bk)])
            P.op("act", lambda e, i=i, pv=pv: e.activation(out=hT[:, :, i * TK:(i + 1) * TK],
                                                           in_=pv.rearrange("p (c t) -> p c t", c=8), func=AF.Copy),
                 reads=[bkey(bk)], writes=[("hT", i)])

    def next_stats(ctx, i):
        junk = ctx["junk"]
        P.op("act", lambda e: e.activation(out=junk[:, :], in_=x_res[:, i, :], func=AF.Square, accum_out=ssq[:, i:i + 1]),
             reads=[("x", i)], writes=[ctx["kj"], ("ssq", i)])
        P.op("act", lambda e: e.activation(out=rstd[:, i:i + 1], in_=ssq[:, i:i + 1], func=AF.Sqrt, bias=EPS, scale=1.0 / D),
             reads=[("ssq", i)], writes=[("rstd", i)])

    def next_xs(ctx, i):
        xs, g_bc = ctx["xs"], ctx["g_bc"]
        s_ = i % len(xs)
        P.op("dve", lambda e: e.reciprocal(out=rstd[:, i:i + 1], in_=rstd[:, i:i + 1]), reads=[("rstd", i)], writes=[("rstd", i)])
        P.op("dve", lambda e: e.scalar_tensor_tensor(out=xs[s_][:, :], in0=x_res[:, i, :], scalar=rstd[:, i:i + 1],
                                                     in1=g_bc[:, :], op0=ALU.mult, op1=ALU.mult),
             reads=[("x", i), ("rstd", i), ctx["kg"]], writes=[ctx["kx"][s_]])

    def final_stats(ctx, i):
        junk = ctx["junk"]
        P.op("act", lambda e: e.activation(out=junk[:, :], in_=x_res[:, i, :], func=AF.Square, accum_out=ssq[:, i:i + 1]),
             reads=[("x", i)], writes=[ctx["kj"], ("ssq", i)])
        P.op("act", lambda e: e.activation(out=rstd[:, i:i + 1], in_=ssq[:, i:i + 1], func=AF.Sqrt, bias=EPS, scale=1.0 / D),
             reads=[("ssq", i)], writes=[("rstd", i)])

    def final_finish(ctx, i):
        g_bc = ctx["g_bc"]
        P.op("dve", lambda e: e.reciprocal(out=rstd[:, i:i + 1], in_=rstd[:, i:i + 1]), reads=[("rstd", i)], writes=[("rstd", i)])
        P.op("dve", lambda e: e.scalar_tensor_tensor(out=x_res[:, i, :], in0=x_res[:, i, :], scalar=rstd[:, i:i + 1],
                                                     in1=g_bc[:, :], op0=ALU.mult, op1=ALU.mult),
             reads=[("x", i), ("rstd", i), ctx["kg"]], writes=[("x", i)])
        P.op("sp", lambda e: e.dma_start(out=out_d[i * TK:(i + 1) * TK, :], in_=x_res[:, i, :]), reads=[("x", i)], dma=True)

    def do_layer(li, l):
        nk, nq = NK[l], NQ[l]
        W = L[l]
        ksbs = _sbs(nk)
        qsbs = _sbs(nq)
        phase_begin()
        TA = nk * TK + 2 * PADA
        hA = WS("hA%d" % l, [128, 2, TA], BF16)
        diagA = WS("diagA%d" % l, [128, 2, 31, 128], BF16)
        caw = WS("caw%d" % l, [128, 2, 31], F32)
        vec = WS("vecA%d" % l, [128, 6], F32)
        var_sb = WS("var%d" % l, [128, 512], F32)
        sig = var_sb
        P.op("sp", lambda e: e.dma_start(out=caw[:, :, :], in_=W["caw"]), writes=[P.K("caw")], dma=True)
        P.op("sp", lambda e: e.dma_start(out=vec[:, 0:2], in_=W["cab"]), writes=[P.K("vec")], dma=True)
        P.op("sp", lambda e: e.dma_start(out=vec[:, 2:4], in_=W["lng"]), writes=[P.K("vec")], dma=True)
        P.op("sp", lambda e: e.dma_start(out=vec[:, 4:6], in_=W["lnb"]), writes=[P.K("vec")], dma=True)
        P.op("pool", lambda e: e.memset(hA[:, :, 0:PADA], 0.0), writes=[P.K("hA", 0), P.K("hA", 1)])
        P.op("pool", lambda e: e.memset(hA[:, :, PADA + nk * TK:TA], 0.0), writes=[P.K("hA", 0), P.K("hA", 1)])
        slot = advance(l, "A_key")
        diag_todo = [(c, j) for c in range(2) for j in range(31)]

        def build_diag(nmax):
            for _ in range(nmax):
                if not diag_todo:
                    return
                c, j = diag_todo.pop(0)
                P.op("dve", lambda e, c=c, j=j: e.tensor_scalar(out=diagA[:, c, j, :], in0=identf[:, :],
                                                                 scalar1=caw[:, c, j:j + 1], scalar2=None, op0=ALU.mult),
                     reads=["identf", P.K("caw")], writes=[P.K("diagA", c, j)])

        def a_key(t0, n):
            for c in range(2):
                bu, bv = gbank(), gbank()
                proj_fm(slot, c, t0, n, bu)
                proj_fm(slot, 2 + c, t0, n, bv)
                P.op("act", lambda e, bv=bv, n=n: e.activation(out=sig[:, 0:n], in_=gen[bv][:, 0:n], func=AF.Sigmoid),
                     reads=[bkey(bv)], writes=[P.K("var")])
                P.op("dve", lambda e, c=c, bu=bu, t0=t0, n=n: e.tensor_tensor(out=hA[:, c, PADA + t0:PADA + t0 + n],
                                                                              in0=gen[bu][:, 0:n], in1=sig[:, 0:n], op=ALU.mult),
                     reads=[bkey(bu), P.K("var")], writes=[P.K("hA", c)])

        if li == 0:
            child = ExitStack()
            nctx = norm_begin(l, False, child, nxs=4)
            nctx["copy_alt"] = True
            tls = [list(range(t0 // TK, (t0 + n) // TK)) for (t0, n) in ksbs]
            for t_ in tls[0]:
                norm_batch(nctx, [t_])
            for k_, (t0, n) in enumerate(ksbs):
                if k_ + 1 < len(ksbs):
                    for t_ in tls[k_ + 1]:
                        next_stats(nctx, t_)
                        next_xs(nctx, t_)
                a_key(t0, n)
                if k_ + 1 < len(ksbs):
                    norm_batch(nctx, tls[k_ + 1], "b")
                    norm_batch(nctx, tls[k_ + 1], "c")
            child.close()
            P.soft_fence()
        else:
            for (t0, n) in ksbs:
                a_key(t0, n)
                build_diag(16)
        build_diag(99)
        slot = advance(l, "A_q")
        szA = [WS("szA%d_%d" % (l, c), [128, 512], BF16) for c in range(2)]
        cf = [WS("cf%d_%d" % (l, c), [128, 512], F32) for c in range(2)]
        cbf = [WS("cbf%d_%d" % (l, c), [128, 512], BF16) for c in range(2)]
        csq = [WS("csq%d_%d" % (l, c), [128, 512], BF16) for c in range(2)]
        for (t0, n) in qsbs:
            for c in range(2):
                bz = gbank()
                proj_fm(slot, c, t0, n, bz)
                P.op("act", lambda e, c=c, bz=bz, n=n: e.activation(out=szA[c][:, 0:n], in_=gen[bz][:, 0:n], func=AF.Silu),
                     reads=[bkey(bz)], writes=[P.K("szA", c)])
            for c in range(2):
                bc = gbank()
                for j in range(31):
                    o = PADA + t0 + j - 15
                    P.op("pe", lambda e, c=c, j=j, o=o, bc=bc, n=n: e.matmul(gen[bc][:, 0:n], lhsT=diagA[:, c, j, :],
                                                                             rhs=hA[:, c, o:o + n], start=(j == 0), stop=(j == 30)),
                         reads=[P.K("diagA", c, j), P.K("hA", c)], writes=[bkey(bc)])
                P.op("act", lambda e, c=c, bc=bc, n=n: e.activation(out=cf[c][:, 0:n], in_=gen[bc][:, 0:n], func=AF.Identity,
                                                                    bias=vec[:, c:c + 1]),
                     reads=[bkey(bc), P.K("vec")], writes=[P.K("cf", c)])
                P.op("act", lambda e, c=c, bc=bc, n=n: e.activation(out=cbf[c][:, 0:n], in_=gen[bc][:, 0:n], func=AF.Identity,
                                                                    bias=vec[:, c:c + 1]),
                     reads=[bkey(bc), P.K("vec")], writes=[P.K("cbf", c)])
                P.op("act", lambda e, c=c, bc=bc, n=n: e.activation(out=csq[c][:, 0:n], in_=gen[bc][:, 0:n], func=AF.Square,
                                                                    bias=vec[:, c:c + 1]),
                     reads=[bkey(bc), P.K("vec")], writes=[P.K("csq", c)])
            bm, be = gbank(), gbank()
            for c in range(2):
                P.op("pe", lambda e, c=c, bm=bm, n=n: e.matmul(gen[bm][:, 0:n], lhsT=ones256[:, :], rhs=cbf[c][:, 0:n],
                                                               start=(c == 0), stop=(c == 1)),
                     reads=["ones256", P.K("cbf", c)], writes=[bkey(bm)])
            for c in range(2):
                P.op("pe", lambda e, c=c, be=be, n=n: e.matmul(gen[be][:, 0:n], lhsT=ones256[:, :], rhs=csq[c][:, 0:n],
                                                               start=(c == 0), stop=(c == 1)),
                     reads=["ones256", P.K("csq", c)], writes=[bkey(be)])
            P.op("act", lambda e, bm=bm, n=n: e.activation(out=var_sb[:, 0:n], in_=gen[bm][:, 0:n], func=AF.Square),
                 reads=[bkey(bm)], writes=[P.K("var")])
            P.op("dve", lambda e, be=be, n=n: e.tensor_tensor(out=var_sb[:, 0:n], in0=gen[be][:, 0:n], in1=var_sb[:, 0:n],
                                                              op=ALU.subtract),
                 reads=[bkey(be), P.K("var")], writes=[P.K("var")])
            P.op("dve", lambda e, n=n: e.tensor_scalar(out=var_sb[:, 0:n], in0=var_sb[:, 0:n], scalar1=0.0, scalar2=None,
                                                       op0=ALU.max),
                 reads=[P.K("var")], writes=[P.K("var")])
            P.op("act", lambda e, n=n: e.activation(out=var_sb[:, 0:n], in_=var_sb[:, 0:n], func=AF.Sqrt, bias=EPS, scale=1.0),
                 reads=[P.K("var")], writes=[P.K("var")])
            P.op("dve", lambda e, n=n: e.reciprocal(out=var_sb[:, 0:n], in_=var_sb[:, 0:n]),
                 reads=[P.K("var")], writes=[P.K("var")])
            for c in range(2):
                P.op("dve", lambda e, c=c, n=n, bm=bm: e.tensor_tensor(out=cf[c][:, 0:n], in0=cf[c][:, 0:n], in1=gen[bm][:, 0:n],
                                                                       op=ALU.subtract),
                     reads=[P.K("cf", c), bkey(bm)], writes=[P.K("cf", c)])
                P.op("dve", lambda e, c=c, n=n: e.tensor_tensor(out=cf[c][:, 0:n], in0=cf[c][:, 0:n], in1=var_sb[:, 0:n],
                                                                op=ALU.mult),
                     reads=[P.K("cf", c), P.K("var")], writes=[P.K("cf", c)])
                P.op("act", lambda e, c=c, t0=t0, n=n: e.activation(out=yT[:, c, t0:t0 + n], in_=cf[c][:, 0:n], func=AF.Silu,
                                                                    scale=vec[:, 2 + c:3 + c], bias=vec[:, 4 + c:5 + c]),
                     reads=[P.K("cf", c), P.K("vec")], writes=[("yT", c)])
                P.op("dve", lambda e, c=c, t0=t0, n=n: e.tensor_tensor(out=yT[:, c, t0:t0 + n], in0=yT[:, c, t0:t0 + n],
                                                                        in1=szA[c][:, 0:n], op=ALU.mult),
                     reads=[("yT", c), P.K("szA", c)], writes=[("yT", c)])

        phase_begin()
        TB = nk * TK + 2 * PADB
        cx = WS("cx%d" % l, [128, 2, TB], BF16)
        diagB = WS("diagB%d" % l, [128, 2, 3, 128], BF16)
        cbw = WS("cbw%d" % l, [128, 2, 3], F32)
        csb = WS("csb%d" % l, [128, 512], F32)
        P.op("sp", lambda e: e.dma_start(out=cbw[:, :, :], in_=W["cbw"]), writes=[P.K("cbw")], dma=True)
        P.op("pool", lambda e: e.memset(cx[:, :, 0:PADB], 0.0), writes=[P.K("cx", 0), P.K("cx", 1)])
        P.op("pool", lambda e: e.memset(cx[:, :, PADB + nk * TK:TB], 0.0), writes=[P.K("cx", 0), P.K("cx", 1)])
        for c in range(2):
            for j in range(3):
                P.op("dve", lambda e, c=c, j=j: e.tensor_scalar(out=diagB[:, c, j, :], in0=identf[:, :],
                                                                 scalar1=cbw[:, c, j:j + 1], scalar2=None, op0=ALU.mult),
                     reads=["identf", P.K("cbw")], writes=[P.K("diagB", c)])
        slot = advance(l, "B_key")
        for (t0, n) in _sbs(min(nk, nq + 1)):
            for c in range(2):
                b1, b2 = gbank(), gbank()
                proj_fm(slot, c, t0, n, b1)
                proj_fm(slot, 2 + c, t0, n, b2)
                P.op("act", lambda e, b1=b1, n=n: e.activation(out=csb[:, 0:n], in_=gen[b1][:, 0:n], func=AF.Copy),
                     reads=[bkey(b1)], writes=[P.K("csb")])
                P.op("dve", lambda e, c=c, b2=b2, t0=t0, n=n: e.tensor_tensor(out=cx[:, c, PADB + t0:PADB + t0 + n],
                                                                              in0=gen[b2][:, 0:n], in1=csb[:, 0:n], op=ALU.mult),
                     reads=[bkey(b2), P.K("csb")], writes=[P.K("cx", c)])
        slot = advance(l, "B_q")
        load_wo(l, 0)
        bsb = [WS("bsb%d_%d" % (l, c), [128, 512], F32) for c in range(2)]
        szB = [WS("szB%d_%d" % (l, c), [128, 512], F32) for c in range(2)]
        for (t0, n) in qsbs:
            for c in range(2):
                b1, b2 = gbank(), gbank()
                proj_fm(slot, c, t0, n, b1)
                proj_fm(slot, 2 + c, t0, n, b2)
                P.op("act", lambda e, c=c, b1=b1, n=n: e.activation(out=bsb[c][:, 0:n], in_=gen[b1][:, 0:n], func=AF.Copy),
                     reads=[bkey(b1)], writes=[P.K("bsb", c)])
                P.op("act", lambda e, c=c, b2=b2, n=n: e.activation(out=szB[c][:, 0:n], in_=gen[b2][:, 0:n], func=AF.Silu),
                     reads=[bkey(b2)], writes=[P.K("szB", c)])
                P.op("dve", lambda e, c=c, n=n: e.tensor_tensor(out=bsb[c][:, 0:n], in0=bsb[c][:, 0:n], in1=szB[c][:, 0:n],
                                                                 op=ALU.mult),
                     reads=[P.K("bsb", c), P.K("szB", c)], writes=[P.K("bsb", c)])
                bc = gbank()
                for j in range(3):
                    o = PADB + t0 + j - 1
                    P.op("pe", lambda e, c=c, j=j, o=o, bc=bc, n=n: e.matmul(gen[bc][:, 0:n], lhsT=diagB[:, c, j, :],
                                                                             rhs=cx[:, c, o:o + n], start=(j == 0), stop=(j == 2)),
                         reads=[P.K("diagB", c), P.K("cx", c)], writes=[bkey(bc)])
                P.op("dve", lambda e, c=c, bc=bc, t0=t0, n=n: e.tensor_tensor(out=yT[:, 2 + c, t0:t0 + n], in0=gen[bc][:, 0:n],
                                                                              in1=bsb[c][:, 0:n], op=ALU.mult),
                     reads=[bkey(bc), P.K("bsb", c)], writes=[("yT", 2 + c)])

        def outproj_tile(i, banks=None, halves=(0, 1)):
            for half in halves:
                bk = gbank() if banks is None else banks[half]
                for c in range(4):
                    P.op("pe", lambda e, c=c, half=half, bk=bk: e.matmul(
                        gen[bk][:, :], lhsT=yT[:, c, i * TK:(i + 1) * TK], rhs=wo[:, c, half * 512:(half + 1) * 512],
                        start=(c == 0), stop=(c == 3)),
                        reads=[("yT", c), "wo"], writes=[bkey(bk)])
                P.op("dve", lambda e, half=half, bk=bk: e.tensor_tensor(
                    out=x_res[:, i, half * 512:(half + 1) * 512], in0=gen[bk][:, :],
                    in1=x_res[:, i, half * 512:(half + 1) * 512], op=ALU.add),
                    reads=[bkey(bk), ("x", i)], writes=[("x", i)])

        def outproj(pair, after_sb=None, fine_last=False, per_tile=False):
            for k_, (t0_, n_) in enumerate(qsbs):
                tl_ = list(range(t0_ // TK, (t0_ + n_) // TK))
                if (per_tile or (fine_last and k_ == len(qsbs) - 1)) and after_sb is not None:
                    for i in tl_:
                        outproj_tile(i)
                        after_sb([i])
                    continue
                for i in tl_:
                    outproj_tile(i)
                if after_sb is not None:
                    after_sb(tl_)


        phase_begin()
        kr = WS("kr%d" % l, [128, nk * TK], BF16)
        Vc = WS("Vc%d" % l, [128, nk, 2, 65], BF16)
        rcs = [WS("rcs%d_%d" % (l, i), [128, 2, 512], F32) for i in range(2)]
        t1 = WS("t1_%d" % l, [128, 512], F32)
        t2 = WS("t2_%d" % l, [128, 512], F32)
        szC = WS("szC%d" % l, [128, 2, nq * TK], BF16)
        esink = WS("esink%d" % l, [128, 4], F32)
        P.op("sp", lambda e: e.dma_start(out=esink[:, :], in_=W["sink"].broadcast_to([128, 4])), writes=[P.K("esink")], dma=True)
        P.op("act", lambda e: e.activation(out=esink[:, :], in_=esink[:, :], func=AF.Exp), reads=[P.K("esink")], writes=[P.K("esink")])
        P.op("pool", lambda e: e.memset(Vc[:, :, :, 64:65], 1.0), writes=[P.K("Vc", i) for i in range(nk)])
        rcnt = [0]

        def load_rope(t0, n):
            s = rcnt[0] % 2
            rcnt[0] += 1
            P.op("sp", lambda e: e.dma_start(out=rcs[s][:, 0, 0:n], in_=rc_d[:, t0:t0 + n]), writes=[P.K("rcs", s)], dma=True)
            P.op("sp", lambda e: e.dma_start(out=rcs[s][:, 1, 0:n], in_=rs_d[:, t0:t0 + n]), writes=[P.K("rcs", s)], dma=True)
            return s

        xb = WS("xb%d" % l, [128, 512], BF16)

        def swap_mm(bx, bsw, n):
            P.op("act", lambda e: e.activation(out=xb[:, 0:n], in_=gen[bx][:, 0:n], func=AF.Copy),
                 reads=[bkey(bx)], writes=[P.K("xb")])
            P.op("pe", lambda e: e.matmul(gen[bsw][:, 0:n], lhsT=psw[:, :], rhs=xb[:, 0:n], start=True, stop=True),
                 reads=["psw", P.K("xb")], writes=[bkey(bsw)])

        def rope(bx, bsw, s, n, out_ap, out_keys):
            P.op("dve", lambda e: e.tensor_tensor(out=t1[:, 0:n], in0=xb[:, 0:n], in1=rcs[s][:, 0, 0:n], op=ALU.mult),
                 reads=[P.K("xb"), P.K("rcs", s)], writes=[P.K("t1")])
            P.op("dve", lambda e: e.tensor_tensor(out=t2[:, 0:n], in0=gen[bsw][:, 0:n], in1=rcs[s][:, 1, 0:n], op=ALU.mult),
                 reads=[bkey(bsw), P.K("rcs", s)], writes=[P.K("t2")])
            P.op("dve", lambda e: e.tensor_tensor(out=out_ap, in0=t1[:, 0:n], in1=t2[:, 0:n], op=ALU.add),
                 reads=[P.K("t1"), P.K("t2")], writes=out_keys)

        slot = advance(l, "C_key")
        for (t0, n) in _sbs(min(nk, nq + 1)):
            s = load_rope(t0, n)
            bk_, bs_ = gbank(), gbank()
            proj_fm(slot, 0, t0, n, bk_)
            P.op("act", lambda e, bk_=bk_, n=n: e.activation(out=xb[:, 0:n], in_=gen[bk_][:, 0:n], func=AF.Copy),
                 reads=[bkey(bk_)], writes=[P.K("xb")])
            for i in range(t0 // TK, (t0 + n) // TK):
                bv = gbank()
                proj_tm(slot, 128, 128, i, bv)
                P.op("act", lambda e, i=i, bv=bv: e.activation(out=Vc[:, i, :, 0:64],
                                                               in_=gen[bv][:, 0:128].rearrange("p (h d) -> p h d", h=2),
                                                               func=AF.Copy),
                     reads=[bkey(bv)], writes=[P.K("Vc", i)])
            P.op("pe", lambda e, bs_=bs_, n=n: e.matmul(gen[bs_][:, 0:n], lhsT=psw[:, :], rhs=xb[:, 0:n], start=True, stop=True),
                 reads=["psw", P.K("xb")], writes=[bkey(bs_)])
            rope(bk_, bs_, s, n, kr[:, t0:t0 + n], [P.K("kr", t) for t in range(t0 // TK, (t0 + n) // TK)])
        slot = advance(l, "C_z")
        for (t0, n) in qsbs:
            for c in range(2):
                bz = gbank()
                proj_fm(slot, c, t0, n, bz)
                P.op("act", lambda e, c=c, bz=bz, t0=t0, n=n: e.activation(out=szC[:, c, t0:t0 + n], in_=gen[bz][:, 0:n],
                                                                           func=AF.Silu),
                     reads=[bkey(bz)], writes=[P.K("szC", c)])
        slot = advance(l, "C_q")
        gmode["wide"] = False
        NARROW[:] = [0, 7]
        qr = [WS("qr%d_%d" % (l, i), [128, 2, 512], BF16) for i in range(2)]
        PT = [WS("PT%d_%d" % (l, i), [128, 384], BF16) for i in range(3)]
        ytok = [WS("ytokC%d_%d" % (l, i), [128, 256], BF16) for i in range(2)]
        den = WS("denC%d" % l, [128, 4, 1], F32)
        tbC = gen[TB_BANK].bitcast(BF16)

        def c_proj(k):
            t0, n = qsbs[k]
            st = []
            hold = {}

            def s_rope():
                hold["s"] = load_rope(t0, n)
            st.append(s_rope)
            for c in range(2):
                def s_q(c=c):
                    s = hold["s"]
                    bq = gbank()
                    hold["bq", c] = bq
                    proj_fm(slot, c, t0, n, bq)
                    P.op("act", lambda e: e.activation(out=xb[:, 0:n], in_=gen[bq][:, 0:n], func=AF.Copy),
                         reads=[bkey(bq)], writes=[P.K("xb")])
                    P.op("dve", lambda e: e.tensor_tensor(out=t1[:, 0:n], in0=xb[:, 0:n], in1=rcs[s][:, 0, 0:n], op=ALU.mult),
                         reads=[P.K("xb"), P.K("rcs", s)], writes=[P.K("t1")])

                def s_sw(c=c):
                    s = hold["s"]
                    bs_ = gbank()
                    P.op("pe", lambda e: e.matmul(gen[bs_][:, 0:n], lhsT=psw[:, :], rhs=xb[:, 0:n], start=True, stop=True),
                         reads=["psw", P.K("xb")], writes=[bkey(bs_)])
                    P.op("dve", lambda e: e.tensor_tensor(out=t2[:, 0:n], in0=gen[bs_][:, 0:n], in1=rcs[s][:, 1, 0:n], op=ALU.mult),
                         reads=[bkey(bs_), P.K("rcs", s)], writes=[P.K("t2")])
                    P.op("dve", lambda e: e.tensor_tensor(out=qr[k % 2][:, c, 0:n], in0=t1[:, 0:n], in1=t2[:, 0:n], op=ALU.add),
                         reads=[P.K("t1"), P.K("t2")], writes=[P.K("qr", k % 2, c)])
                st.append(s_q)
                st.append(s_sw)
            return st

        items = []
        cnt = [0]
        for k, (t0, n) in enumerate(qsbs):
            ntile = n // TK
            for ti in range(ntile):
                i = t0 // TK + ti
                kbs = [kb for kb in (i - 1, i, i + 1) if 0 <= kb < nk]
                mlo = kbs[0] - (i - 1)
                nkb = len(kbs)
                for hidx, (c, base) in enumerate([(0, 0), (0, 64), (1, 0), (1, 64)]):
                    sc = s0 if base == 0 else s64
                    sk = "s0" if base == 0 else "s64"
                    half = 0
                    scv = sc[:, 0:512]
                    p = cnt[0] % 3
                    cnt[0] += 1
                    it = {}

                    def S(kbs=kbs, base=base, c=c, ti=ti, scv=scv, sk=sk, half=half, k=k):
                        for jj, kb in enumerate(kbs):
                            P.op("pe", lambda e, jj=jj, kb=kb: e.matmul(
                                scv[:, jj * 128:(jj + 1) * 128], lhsT=kr[base:base + 64, kb * TK:(kb + 1) * TK],
                                rhs=qr[k % 2][base:base + 64, c, ti * TK:(ti + 1) * TK], start=True, stop=True),
                                reads=[P.K("kr", kb), P.K("qr", k % 2, c)], writes=[(sk, half)])

                    def E(p=p, scv=scv, nkb=nkb, mlo=mlo, sk=sk, half=half, kbs=kbs, i=i):
                        P.op("act", lambda e: e.activation(out=PT[p][:, 0:nkb * 128], in_=scv[:, 0:nkb * 128],
                                                           func=AF.Exp, scale=0.125),
                             reads=[(sk, half)], writes=[P.K("PT", p)])
                        if SWA_MASK_ON_POOL:
                            for jj, kb in enumerate(kbs):
                                if kb == i:
                                    continue
                                pat, cm = ([[-1, 128]], 1) if kb == i - 1 else ([[1, 128]], -1)
                                P.op("pool", lambda e, jj=jj, pat=pat, cm=cm: e.affine_select(
                                    out=PT[p][:, jj * 128:(jj + 1) * 128], in_=PT[p][:, jj * 128:(jj + 1) * 128], pattern=pat,
                                    compare_op=ALU.is_ge, fill=0.0, base=0, channel_multiplier=cm),
                                    reads=[P.K("PT", p)], writes=[P.K("PT", p)])
                        else:
                            P.op("dve", lambda e: e.tensor_tensor(
                                out=PT[p][:, 0:nkb * 128].rearrange("p (j q) -> p j q", q=128),
                                in0=PT[p][:, 0:nkb * 128].rearrange("p (j q) -> p j q", q=128),
                                in1=mask3[:, mlo:mlo + nkb, :], op=ALU.mult),
                                reads=[P.K("PT", p), "mask3"], writes=[P.K("PT", p)])

                    def PV(kbs=kbs, p=p, hidx=hidx, base=base, nkb=nkb, it=it):
                        bo = it["bo"]
                        kvh = base // 64
                        for jj, kb in enumerate(kbs):
                            P.op("pe", lambda e, jj=jj, kb=kb: e.matmul(
                                gen[bo][:, hidx * 65:(hidx + 1) * 65], lhsT=PT[p][:, jj * 128:(jj + 1) * 128],
                                rhs=Vc[:, kb, kvh, :], start=(jj == 0), stop=(jj == nkb - 1)),
                                reads=[P.K("PT", p), P.K("Vc", kb)], writes=[bkey(bo)])

                    it.update(S=S, E=E, PV=PV, first=(hidx == 0), last=(hidx == 3), ti=ti, k=k, t0=t0, n=n,
                              proj_trigger=(hidx == 0 and ti == 1), last_of_sb=(hidx == 3 and ti == ntile - 1))
                    if hidx == 1:
                        it["filler"] = (lambda i=i: outproj_tile(i, (5, 5), halves=(0,)))
                    if hidx == 3:
                        it["filler"] = (lambda i=i: outproj_tile(i, (5, 5), halves=(1,)))
                    items.append(it)

        def c_post(it):
            bo = it["bo"]
            ob3 = gen[bo][:, 0:260].rearrange("p (h e) -> p h e", e=65)
            y = it["ti"] % 2
            P.op("dve", lambda e: e.tensor_tensor(out=den[:, :, :], in0=ob3[:, :, 64:65],
                                                  in1=esink[:, :].rearrange("p (h o) -> p h o", o=1), op=ALU.add),
                 reads=[bkey(bo), P.K("esink")], writes=[P.K("den")])
            P.op("dve", lambda e: e.reciprocal(out=den[:, :, :], in_=den[:, :, :]), reads=[P.K("den")], writes=[P.K("den")])
            P.op("dve", lambda e: e.tensor_tensor(out=ytok[y][:, :].rearrange("p (h d) -> p h d", d=64),
                                                  in0=ob3[:, :, 0:64], in1=den[:, :, :].broadcast_to([128, 4, 64]),
                                                  op=ALU.mult),
                 reads=[bkey(bo), P.K("den")], writes=[P.K("ytok", y)])

        def c_post_pe(it):
            y = it["ti"] % 2
            ti = it["ti"]
            for c in range(2):
                P.op("pe", lambda e, c=c: e.transpose(
                    out=tbC[:, c * 512 + ti * 128:c * 512 + (ti + 1) * 128], in_=ytok[y][:, c * 128:(c + 1) * 128],
                    identity=ident[:, :]),
                    reads=[P.K("ytok", y), "ident"], writes=[bkey(TB_BANK)])
            if it["last_of_sb"]:
                t0, n = it["t0"], it["n"]
                for c in range(2):
                    P.op("dve", lambda e, c=c: e.tensor_tensor(out=yT[:, c, t0:t0 + n], in0=tbC[:, c * 512:c * 512 + n],
                                                               in1=szC[:, c, t0:t0 + n], op=ALU.mult),
                         reads=[bkey(TB_BANK), P.K("szC", c)], writes=[("yT", c)])

        def c_warm(bo):
            P.op("pe", lambda e: e.matmul(gen[bo][:, 384:512], lhsT=ident[:, :], rhs=ident[:, :], start=True, stop=True),
                 writes=[bkey(bo)])

        run_pipeline(items, c_proj, c_post, c_post_pe, len(qsbs), c_warm, NWARM_C)
        gmode["wide"] = True

        phase_begin()
        kD = WS("kD%d" % l, [128, 2, nk * TK], BF16)
        Vd = WS("Vd%d" % l, [128, nk, 4, 65], BF16)
        etab = WS("etab%d" % l, [128, 5, 4, 128], BF16)
        e01 = x_res[:, 18:20, :].bitcast(BF16).rearrange("p t (j h q) -> p (t j) h q", h=4, q=128)
        P.op("pool", lambda e: e.memset(Vd[:, :, :, 64:65], 1.0), writes=[P.K("Vd", i) for i in range(nk)])
        P.op("pool", lambda e: e.memset(x_res[:, 18:20, :], 0.0), writes=[("x", 18), ("x", 19)] + [("e01", j) for j in range(8)])

        def etab_of(cls):
            if cls == 2:
                return etab, (lambda j: P.K("etab", j)), 0
            return e01, (lambda j: ("e01", j)), 4 * cls

        eprep = []
        for cls in (0, 1, 2):
            dst, kf, joff = etab_of(cls)
            for j in range(NA_CLASS_CH[cls]):
                gi = NA_CLASS_OFF[cls] + j
                jj = joff + j
                rd = []
                P.op("pool", lambda e, gi=gi, dst=dst, jj=jj: e.dma_start(out=dst[:, jj, :, :], in_=W["nab"][:, gi, :, :]),
                     reads=rd, writes=[kf(jj)], dma=True)
                eprep.append((dst, [kf(jj)], jj))

        def emit_eprep(nmax):
            for _ in range(nmax):
                if not eprep:
                    return
                dst, dkeys, jj = eprep.pop(0)
                P.op("act", lambda e, dst=dst, jj=jj: e.activation(out=dst[:, jj, :, :], in_=dst[:, jj, :, :], func=AF.Exp),
                     reads=dkeys, writes=dkeys)
        slot = advance(l, "D_key")
        for (t0, n) in ksbs:
            for c in range(2):
                bk_ = gbank()
                proj_fm(slot, c, t0, n, bk_)
                P.op("act", lambda e, c=c, bk_=bk_, t0=t0, n=n: e.activation(out=kD[:, c, t0:t0 + n], in_=gen[bk_][:, 0:n],
                                                                            func=AF.Copy),
                     reads=[bkey(bk_)], writes=[P.K("kD", c, t) for t in range(t0 // TK, (t0 + n) // TK)])
            for i in range(t0 // TK, (t0 + n) // TK):
                bv = gbank()
                proj_tm(slot, 256, 256, i, bv)
                P.op("act", lambda e, i=i, bv=bv: e.activation(out=Vd[:, i, :, 0:64],
                                                               in_=gen[bv][:, 0:256].rearrange("p (h d) -> p h d", h=4),
                                                               func=AF.Copy),
                     reads=[bkey(bv)], writes=[P.K("Vd", i)])
            emit_eprep(3)
        emit_eprep(99)
        slot = advance(l, "D_q")
        load_wo(l, 1)
        for (t0, n) in qsbs:
            hk = hkeys(t0, n)
            bks = [gbank() for _ in range(4)]
            for b_, bk_ in enumerate(bks):
                proj_fm(slot, b_, t0, n, bk_)
            for c in range(2):
                P.op("act", lambda e, c=c, bk_=bks[c], t0=t0, n=n: e.activation(out=hT[:, c, t0:t0 + n], in_=gen[bk_][:, 0:n],
                                                                               func=AF.Copy),
                     reads=[bkey(bks[c])], writes=hk)
            for c in range(2):
                P.op("act", lambda e, c=c, bk_=bks[2 + c], t0=t0, n=n: e.activation(out=hT[:, 2 + c, t0:t0 + n],
                                                                                   in_=gen[bk_][:, 0:n], func=AF.Silu),
                     reads=[bkey(bks[2 + c])], writes=hk)
        gmode["wide"] = False
        NARROW[:] = [0]
        PTd = [WS("PTd%d_%d" % (l, i), [128, 640], BF16) for i in range(3)]
        ytokD = [WS("ytokD%d_%d" % (l, i), [128, 256], BF16) for i in range(2)]
        denD = WS("denD%d" % l, [128, 4, 1], F32)
        tbD = gen[TB_BANK].bitcast(BF16)

        def d_proj(k):
            return []

        items = []
        cur_cls = [-1]
        dcnt = [0]
        for k, (t0, n) in enumerate(qsbs):
            ntile = n // TK
            for ti in range(ntile):
                i = t0 // TK + ti
                cls = min(i, 2)
                lo = max(0, i - 2)
                hi = min(nk - 1, max(i + 2, 3))
                cl = list(range(lo, hi + 1))
                j0 = lo - (i - 2) if cls == 2 else lo
                ncl = len(cl)
                for h in range(4):
                    c, base = h // 2, (h % 2) * 64
                    sc = s0 if base == 0 else s64
                    sk = "s0" if base == 0 else "s64"
                    p = dcnt[0] % 3
                    dcnt[0] += 1
                    it = {}
                    need_cls = None
                    if h == 0 and cls != cur_cls[0]:
                        need_cls = cls
                        cur_cls[0] = cls

                    def S(cl=cl, base=base, c=c, ti=ti, sc=sc, sk=sk, k=k, i=i):
                        for jj, kc in enumerate(cl):
                            P.op("pe", lambda e, jj=jj, kc=kc: e.matmul(
                                sc[:, jj * 128:(jj + 1) * 128], lhsT=kD[base:base + 64, c, kc * TK:(kc + 1) * TK],
                                rhs=hT[base:base + 64, c, i * TK:(i + 1) * TK], start=True, stop=True),
                                reads=[P.K("kD", c, kc), ("hT", i)], writes=[(sk, 0), (sk, 1)])

                    def E(p=p, sc=sc, ncl=ncl, j0=j0, h=h, sk=sk, cls=cls):
                        etb, kf, joff = etab_of(cls)
                        ekeys = [kf(joff + j0 + jx) for jx in range(ncl)]
                        P.op("act", lambda e: e.activation(out=PTd[p][:, 0:ncl * 128], in_=sc[:, 0:ncl * 128],
                                                           func=AF.Exp, scale=0.125),
                             reads=[(sk, 0), (sk, 1)], writes=[P.K("PTd", p)])
                        P.op("dve", lambda e: e.tensor_tensor(
                            out=PTd[p][:, 0:ncl * 128].rearrange("p (j q) -> p j q", q=128),
                            in0=PTd[p][:, 0:ncl * 128].rearrange("p (j q) -> p j q", q=128),
                            in1=etb[:, joff + j0:joff + j0 + ncl, h, :], op=ALU.mult),
                            reads=[P.K("PTd", p)] + ekeys, writes=[P.K("PTd", p)])

                    def PV(cl=cl, p=p, h=h, ncl=ncl, it=it):
                        bo = it["bo"]
                        for jj, kc in enumerate(cl):
                            P.op("pe", lambda e, jj=jj, kc=kc: e.matmul(
                                gen[bo][:, h * 65:(h + 1) * 65], lhsT=PTd[p][:, jj * 128:(jj + 1) * 128],
                                rhs=Vd[:, kc, h, :], start=(jj == 0), stop=(jj == ncl - 1)),
                                reads=[P.K("PTd", p), P.K("Vd", kc)], writes=[bkey(bo)])

                    it.update(S=S, E=E, PV=PV, first=(h == 0), last=(h == 3), ti=ti, k=k, t0=t0, n=n,
                              proj_trigger=(h == 0 and ti == 1), last_of_sb=(h == 3 and ti == ntile - 1))
                    items.append(it)

        def d_post(it):
            bo = it["bo"]
            ob3 = gen[bo][:, 0:260].rearrange("p (h e) -> p h e", e=65)
            y = it["ti"] % 2
            P.op("dve", lambda e: e.reciprocal(out=denD[:, :, :], in_=ob3[:, :, 64:65]),
                 reads=[bkey(bo)], writes=[P.K("denD")])
            P.op("dve", lambda e: e.tensor_tensor(out=ytokD[y][:, :].rearrange("p (h d) -> p h d", d=64),
                                                  in0=ob3[:, :, 0:64], in1=denD[:, :, :].broadcast_to([128, 4, 64]),
                                                  op=ALU.mult),
                 reads=[bkey(bo), P.K("denD")], writes=[P.K("ytokD", y)])

        def d_post_pe(it):
            y = it["ti"] % 2
            ti = it["ti"]
            k = it["k"]
            for c in range(2):
                P.op("pe", lambda e, c=c: e.transpose(
                    out=tbD[:, c * 512 + ti * 128:c * 512 + (ti + 1) * 128], in_=ytokD[y][:, c * 128:(c + 1) * 128],
                    identity=ident[:, :]),
                    reads=[P.K("ytokD", y), "ident"], writes=[bkey(TB_BANK)])
            if it["last_of_sb"]:
                t0, n = it["t0"], it["n"]
                for c in range(2):
                    P.op("dve", lambda e, c=c: e.tensor_tensor(out=yT[:, 2 + c, t0:t0 + n], in0=tbD[:, c * 512:c * 512 + n],
                                                               in1=hT[:, 2 + c, t0:t0 + n], op=ALU.mult),
                         reads=[bkey(TB_BANK)] + hkeys(t0, n), writes=[("yT", 2 + c)])

        def d_warm(bo):
            P.op("pe", lambda e: e.matmul(gen[bo][:, 384:512], lhsT=ident[:, :], rhs=ident[:, :], start=True, stop=True),
                 writes=[bkey(bo)])

        run_pipeline(items, d_proj, d_post, d_post_pe, len(qsbs), d_warm, NWARM_D)
        gmode["wide"] = True

        nxt = layers[li + 1] if li + 1 < len(layers) else None
        if nxt is not None:
            nctx = norm_begin(nxt, True, nxs=8)
            pend_t, cur_n, pend_n = [], [], []

            def flush_group():
                if pend_n:
                    prev = pend_n.pop(0)
                    norm_batch(nctx, prev, "b")
                    norm_batch(nctx, prev, "c")

            def push_xs(t2):
                next_xs(nctx, t2)
                cur_n.append(t2)
                if len(cur_n) == 4:
                    flush_group()
                    pend_n.append(list(cur_n))
                    del cur_n[:]

            def after_tile_next(tl_):
                for t in tl_:
                    if t < NK[nxt]:
                        next_stats(nctx, t)
                        if pend_t:
                            push_xs(pend_t.pop(0))
                        pend_t.append(t)
            outproj(1, after_sb=after_tile_next, per_tile=True)
            while pend_t:
                push_xs(pend_t.pop(0))
            if cur_n:
                pend_n.append(list(cur_n))
            while pend_n:
                flush_group()
        elif do_final:
            nctx = norm_begin(None, True)
            pend_f = []

            def after_tile_final(tl_):
                for t in tl_:
                    if t < out_tiles:
                        final_stats(nctx, t)
                        if pend_f:
                            final_finish(nctx, pend_f.pop(0))
                        pend_f.append(t)
            outproj(1, after_sb=after_tile_final, per_tile=True)
            while pend_f:
                final_finish(nctx, pend_f.pop(0))
        else:
            phase_begin()
            outproj(1)

    for li_, l_ in enumerate(layers):
        do_layer(li_, l_)

    if not do_final:
        for i in range(out_tiles):
            P.op("sp", lambda e, i=i: e.dma_start(out=out_d[i * TK:(i + 1) * TK, :], in_=x_res[:, i, :]),
                 reads=[("x", i)], dma=True)
    P.emit(final_wait_ops=P.dma_ops)
    return nc, P


def _rope_tables(ty):
    t = np.arange(NT_MAX * TK)
    pos = (t if ty == 0 else 4095 - t).astype(np.float32)
    inv_freq = (np.float32(10000.0) ** (-np.arange(0, 64, 2, dtype=np.float32) / np.float32(64))).astype(np.float32)
    ang = (pos[:, None] * inv_freq[None, :]).astype(np.float32)
    cos = np.cos(ang).astype(np.float32)
    sin = np.sin(ang).astype(np.float32)
    p = np.arange(128)
    d = p % 64
    fi = d % 32
    sgn = np.where(d < 32, -1.0, 1.0).astype(np.float32)
    cosT = np.ascontiguousarray(cos[:, fi].T)
    sinT = np.ascontiguousarray((sin[:, fi] * sgn[None, :]).T)
    return cosT, sinT


def _na_index(ty):
    valid = np.zeros((128, NA_NCH, 128), dtype=bool)
    dr = np.zeros((128, NA_NCH, 128), dtype=np.int64)
    dc = np.zeros((128, NA_NCH, 128), dtype=np.int64)
    kk = np.arange(128)
    qq = np.arange(128)
    for cls in range(3):
        for j in range(NA_CLASS_CH[cls]):
            gi = NA_CLASS_OFF[cls] + j
            qrow = 2 * cls + qq // 64
            qcol = qq % 64
            krow = 2 * j + kk // 64
            kcol = kk % 64
            if ty == 1:
                qrow, qcol, krow, kcol = 63 - qrow, 63 - qcol, 63 - krow, 63 - kcol
            r0 = np.clip(qrow - 4, 0, 56)
            c0 = np.clip(qcol - 8, 0, 48)
            vr = (krow[:, None] >= r0[None, :]) & (krow[:, None] < r0[None, :] + 8)
            vc = (kcol[:, None] >= c0[None, :]) & (kcol[:, None] < c0[None, :] + 16)
            valid[:, gi, :] = vr & vc
            dr[:, gi, :] = np.clip(krow[:, None] - qrow[None, :] + 7, 0, 14)
            dc[:, gi, :] = np.clip(kcol[:, None] - qcol[None, :], -15, 15) + 15
    return valid, dr, dc


_NA_IDX = {}
_mm = np.arange(128)
_PSW = np.zeros((128, 128), np.float32)
_PSW[(_mm // 64) * 64 + (_mm % 64 + 32) % 64, _mm] = 1.0


def _prep_core_inputs(core, inputs, layers):
    seq, ty = core // 2, core % 2
    x = inputs["x"]
    if ty == 0:
        xc = x[seq, 0:NT_MAX * TK]
    else:
        xc = x[seq, ::-1][0:NT_MAX * TK]
    m = {"x": np.ascontiguousarray(xc, dtype=np.float32)}
    cosT, sinT = _rope_tables(ty)
    m["rope_cos"], m["rope_sin"] = cosT, sinT
    if ty not in _NA_IDX:
        _NA_IDX[ty] = _na_index(ty)
    valid, dr, dc = _NA_IDX[ty]
    m["fg"] = np.ascontiguousarray(inputs["final_norm_g"].reshape(1, D))
    m["psw"] = _PSW
    for l in layers:
        m["win%d" % l] = np.ascontiguousarray(inputs["w_in"][l][:, COLPERM])
        m["wout%d" % l] = np.ascontiguousarray(inputs["w_out"][l][ROWPERM, :])
        m["ng%d" % l] = np.ascontiguousarray(inputs["norm_g"][l].reshape(1, D))
        caw = inputs["conv_a_w"][l]
        cbw = inputs["conv_b_w"][l]
        if ty == 1:
            caw, cbw = caw[::-1], cbw[::-1]
        m["caw%d" % l] = np.ascontiguousarray(caw.T.reshape(2, 128, 31).transpose(1, 0, 2))
        m["cbw%d" % l] = np.ascontiguousarray(cbw.T.reshape(2, 128, 3).transpose(1, 0, 2))
        m["cab%d" % l] = np.ascontiguousarray(inputs["conv_a_b"][l].reshape(2, 128).T)
        m["lng%d" % l] = np.ascontiguousarray(inputs["ln_a_g"][l].reshape(2, 128).T)
        m["lnb%d" % l] = np.ascontiguousarray(inputs["ln_a_b"][l].reshape(2, 128).T)
        m["sink%d" % l] = np.ascontiguousarray(inputs["swa_sink"][l][SINK_ORDER].reshape(1, 4))
        rpb = inputs["na_rpb"][l]
        g = rpb[:, dr, dc]
        g = np.where(valid[None], g, np.float32(-30000.0))
        m["nab%d" % l] = np.ascontiguousarray(g.transpose(1, 2, 0, 3).astype(np.float32))
    return m


_PROGS = {}


def _get_prog(key, **kw):
    if key not in _PROGS:
        _PROGS[key] = build_program(**kw)[0]
    return _PROGS[key]


FUSED = True
PIPE_DEPTH = 2
SWA_MASK_ON_POOL = True
C_MASK_SPLIT = False
SSQ_ON_DVE = False
PE_DEFER = 2
NWARM_C = 0
NWARM_D = 0


def kernel(**inputs):
    inputs = {k: np.asarray(v) for k, v in inputs.items()}
    ncore = 8
    if FUSED:
        nc = _get_prog("fused", layers=(0, 1), do_final=True, x_in_tiles=NK[0], out_tiles=NOUT)
        maps = [_prep_core_inputs(c, inputs, (0, 1)) for c in range(ncore)]
        res = run_bass_kernel_spmd(nc, maps, core_ids=list(range(ncore)))
        outs = [r["out"] for r in res.results]
    else:
        nc1 = _get_prog("l0", layers=(0,), do_final=False, x_in_tiles=NK[0], out_tiles=NQ[0])
        maps = [_prep_core_inputs(c, inputs, (0,)) for c in range(ncore)]
        res1 = run_bass_kernel_spmd(nc1, maps, core_ids=list(range(ncore)))
        nc2 = _get_prog("l1", layers=(1,), do_final=True, x_in_tiles=NK[1], out_tiles=NOUT)
        maps2 = []
        for c in range(ncore):
            m = _prep_core_inputs(c, inputs, (1,))
            m["x"] = np.ascontiguousarray(res1.results[c]["out"])
            maps2.append(m)
        res2 = run_bass_kernel_spmd(nc2, maps2, core_ids=list(range(ncore)))
        outs = [r["out"] for r in res2.results]
    B, S = inputs["x"].shape[0], inputs["x"].shape[1]
    out = np.empty((B, S, D), dtype=np.float32)
    for c in range(ncore):
        seq, ty = c // 2, c % 2
        o = np.asarray(outs[c]).reshape(NOUT * TK, D)
        if ty == 0:
            out[seq, 0:NOUT * TK] = o
        else:
            out[seq, S - NOUT * TK:] = o[::-1]
    return out
```

```python
import numpy as np
import concourse.bass as bass
import concourse.mybir as mybir
from contextlib import ExitStack
from concourse.bass_utils import run_bass_kernel_spmd

F32 = mybir.dt.float32
BF16 = mybir.dt.bfloat16
ALU = mybir.AluOpType
AF = mybir.ActivationFunctionType


STRICT_SAME_ENGINE = True


class _Op:
    __slots__ = ("eng", "fn", "deps", "is_dma", "needs_inc", "tok")

    def __init__(self, eng, fn, is_dma):
        self.eng = eng
        self.fn = fn
        self.deps = []
        self.is_dma = is_dma
        self.needs_inc = is_dma
        self.tok = None


class _St:
    __slots__ = ("w", "rd")

    def __init__(self):
        self.w = None
        self.rd = []


class Prog:
    ENGS = ("pe", "act", "dve", "pool", "sp")
    NDMA = {"sp": 8, "pool": 4, "act": 2}

    def __init__(self, nc):
        self.nc = nc
        self.ops = {e: [] for e in self.ENGS}
        self.state = {}
        self.dma_last = {q: [None] * n for q, n in self.NDMA.items()}
        self.dma_cnt = {q: 0 for q in self.NDMA}
        self.dma_ops = []
        self.cur_fence = []
        self.phase = 0
        self.last_ws = {}
        self.dma_last_ws = {q: [None] * n for q, n in self.NDMA.items()}

    def fence(self):
        f = [o for o in self.last_ws.values()]
        for q in self.NDMA:
            for o in self.dma_last_ws[q]:
                if o is not None:
                    f.append(o)
        self.cur_fence = f
        self.phase += 1

    def soft_fence(self):
        ph = self.phase
        self.fence()
        self.phase = ph

    def K(self, *a):
        return ("ws", self.phase) + a

    def _get(self, k):
        st = self.state.get(k)
        if st is None:
            st = self.state[k] = _St()
            if isinstance(k, tuple) and k and k[0] == "ws":
                st.rd = list(self.cur_fence)
        return st

    def _add_dep(self, o, p, raw):
        if p is None or p is o:
            return
        if (not o.is_dma) and (not p.is_dma) and p.eng == o.eng:
            if o.eng == "pe" or (not raw and not STRICT_SAME_ENGINE):
                return
        o.deps.append(p)

    def op(self, eng, fn, reads=(), writes=(), dma=False):
        o = _Op(eng, fn, dma)
        for k in reads:
            st = self._get(k)
            self._add_dep(o, st.w, True)
        for k in writes:
            st = self._get(k)
            self._add_dep(o, st.w, False)
            for r in st.rd:
                self._add_dep(o, r, False)
        if dma:
            q = eng
            slot = self.dma_cnt[q] % self.NDMA[q]
            self.dma_cnt[q] += 1
            prev = self.dma_last[q][slot]
            if prev is not None:
                o.deps.append(prev)
            self.dma_last[q][slot] = o
            o.tok = (q, slot)
            self.dma_ops.append(o)
        for k in reads:
            st = self.state[k]
            if not dma:
                st.rd = [r for r in st.rd if r.is_dma or r.eng != eng]
            st.rd.append(o)
        for k in writes:
            st = self.state[k]
            st.w = o
            st.rd = []
        for p in o.deps:
            p.needs_inc = True
        self.ops[eng].append(o)
        if any(isinstance(k, tuple) and k and k[0] == "ws" for k in list(reads) + list(writes)):
            if dma:
                self.dma_last_ws[eng][slot] = o
            else:
                self.last_ws[eng] = o
        return o

    def emit(self, final_wait_ops=()):
        nc = self.nc
        sems = {e: nc.alloc_semaphore("s_" + e) for e in ("pe", "act", "dve", "pool")}
        dsems = {q: [nc.alloc_semaphore("d_%s%d" % (q, i)) for i in range(n)] for q, n in self.NDMA.items()}
        for e in self.ENGS:
            cnt = 0
            dcnt = {}
            for o in self.ops[e]:
                if o.is_dma:
                    q, slot = o.tok
                    dcnt[(q, slot)] = dcnt.get((q, slot), 0) + 16
                    o.tok = (dsems[q][slot], dcnt[(q, slot)])
                elif o.needs_inc:
                    cnt += 1
                    o.tok = (sems[e], cnt)
        self.stats = {}
        with nc.Block() as block:
            def mk(e):
                def body(eng):
                    waited = {}
                    nw = 0
                    for o in self.ops[e]:
                        for p in o.deps:
                            s, v = p.tok
                            if waited.get(s.num, 0) < v:
                                eng.wait_ge(s, v)
                                waited[s.num] = v
                                nw += 1
                        ins = o.fn(eng)
                        if o.is_dma:
                            ins.then_inc(o.tok[0], 16)
                        elif o.needs_inc:
                            ins.then_inc(o.tok[0], 1)
                    if e == "sp":
                        for p in final_wait_ops:
                            s, v = p.tok
                            if waited.get(s.num, 0) < v:
                                eng.wait_ge(s, v)
                                waited[s.num] = v
                    self.stats[e] = (len(self.ops[e]), nw)
                return body
            block.tensor(mk("pe"))
            block.scalar(mk("act"))
            block.vector(mk("dve"))
            block.gpsimd(mk("pool"))
            block.sync(mk("sp"))


D = 1024
TK = 128
NT_MAX = 20
NK = (20, 18)
NQ = (18, 16)
NOUT = 16
EPS = 1e-6
PADA = 16
PADB = 2

_O = dict(a_u=0, a_v=256, a_z=512, b_b=768, b_c=1024, b_x=1280, b_z=1536,
          c_q=1792, c_k=2048, c_v=2176, c_z=2304, d_q=2560, d_k=2816, d_v=3072, d_z=3328)


def _swap64(cols):
    cols = np.asarray(cols).reshape(-1, 64)
    return np.concatenate([cols[:, 32:], cols[:, :32]], axis=1).reshape(-1)


def _build_units():
    r = np.arange
    cq = _O["c_q"]
    cz = _O["c_z"]
    hq = lambda h: cq + h * 64 + r(64)
    hz = lambda h: cz + h * 64 + r(64)
    q0 = np.concatenate([hq(0), hq(2)])
    q1 = np.concatenate([hq(1), hq(3)])
    z0 = np.concatenate([hz(0), hz(2)])
    z1 = np.concatenate([hz(1), hz(3)])
    ck = _O["c_k"] + r(128)
    units = [
        ("A_key", np.concatenate([_O["a_u"] + r(256), _O["a_v"] + r(256)])),
        ("A_q", _O["a_z"] + r(256)),
        ("B_key", np.concatenate([_O["b_c"] + r(256), _O["b_x"] + r(256)])),
        ("B_q", np.concatenate([_O["b_b"] + r(256), _O["b_z"] + r(256)])),
        ("C_key", np.concatenate([ck, _O["c_v"] + r(128)])),
        ("C_z", np.concatenate([z0, z1])),
        ("C_q", np.concatenate([q0, q1])),
        ("D_key", np.concatenate([_O["d_k"] + r(256), _O["d_v"] + r(256)])),
        ("D_q", np.concatenate([_O["d_q"] + r(256), _O["d_z"] + r(256)])),
    ]
    return units


UNITS = _build_units()
UNIT_OFF = {}
_off = 0
for _n, _c in UNITS:
    UNIT_OFF[_n] = (_off, len(_c) // 128)
    _off += len(_c)
NCOLS = _off
COLPERM = np.concatenate([c for _, c in UNITS])
_c0 = 512
ROWPERM = np.concatenate([np.arange(0, 512),
                          _c0 + 0 * 64 + np.arange(64), _c0 + 2 * 64 + np.arange(64),
                          _c0 + 1 * 64 + np.arange(64), _c0 + 3 * 64 + np.arange(64),
                          np.arange(768, 1024)])
SINK_ORDER = [0, 2, 1, 3]
NA_CLASS_CH = (4, 4, 5)
NA_CLASS_OFF = (0, 4, 8)
NA_NCH = 13


def _sbs(ntiles):
    out = []
    t = 0
    while t < ntiles:
        n = min(4, ntiles - t)
        out.append((t * TK, n * TK))
        t += n
    return out


def build_program(layers=(0, 1), do_final=True, x_in_tiles=20, out_tiles=16):
    nc = bass.Bass("TRN2", target_bir_lowering=False)
    P = Prog(nc)
    dt_in = lambda name, shape, dt=F32: nc.dram_tensor(name, shape, dt, kind="ExternalInput").ap()

    x_d = dt_in("x", [x_in_tiles * TK, D])
    rc_d = dt_in("rope_cos", [128, NT_MAX * TK])
    rs_d = dt_in("rope_sin", [128, NT_MAX * TK])
    fg_d = dt_in("fg", [1, D])
    psw_d = dt_in("psw", [128, 128])
    L = {}
    for l in layers:
        L[l] = dict(
            win=dt_in("win%d" % l, [D, NCOLS]), wout=dt_in("wout%d" % l, [D, D]), ng=dt_in("ng%d" % l, [1, D]),
            caw=dt_in("caw%d" % l, [128, 2, 31]), cab=dt_in("cab%d" % l, [128, 2]), lng=dt_in("lng%d" % l, [128, 2]),
            lnb=dt_in("lnb%d" % l, [128, 2]), cbw=dt_in("cbw%d" % l, [128, 2, 3]), sink=dt_in("sink%d" % l, [1, 4]),
            nab=dt_in("nab%d" % l, [128, NA_NCH, 4, 128]))
    out_d = nc.dram_tensor("out", [out_tiles * TK, D], F32, kind="ExternalOutput").ap()

    A = nc.alloc_sbuf_tensor
    ws = {"es": None}

    def phase_begin():
        if ws["es"] is not None:
            ws["es"].close()
        ws["es"] = ExitStack()
        P.fence()

    def WS(name, shape, dt):
        return ws["es"].enter_context(nc.sbuf_tensor("sb_" + name, shape, dt))
    x_res = A("x_res", [128, NT_MAX, D], F32)
    hT = A("hT", [128, 8, NT_MAX * TK], BF16)
    yT = A("yT", [128, 4, NQ[0] * TK], BF16)
    wring = [A("wring%d" % i, [128, 8, 512], BF16) for i in range(2)]
    wo = A("wo", [128, 4, D], BF16)
    ident = A("ident", [128, 128], BF16)
    identf = A("identf", [128, 128], F32)
    ones256 = A("ones256", [128, 128], BF16)
    mask3 = A("mask3", [128, 3, 128], BF16)
    ssq = A("ssq", [128, NT_MAX], F32)
    psw = A("psw_sb", [128, 128], BF16)
    rstd = A("rstd", [128, NT_MAX], F32)
    s0 = nc.alloc_psum_tensor("ps_s0", [128, 1024], F32)
    s64 = nc.alloc_psum_tensor("ps_s64", [128, 1024], F32)
    gen_t = [nc.alloc_psum_tensor("ps_g%d" % i, [128, 512], F32) for i in range(4)]
    gen = [gen_t[0][:, :], gen_t[1][:, :], gen_t[2][:, :], gen_t[3][:, :],
           s0[:, 0:512], s0[:, 512:1024], s64[:, 0:512], s64[:, 512:1024]]
    BKEYS = [("g", 0), ("g", 1), ("g", 2), ("g", 3), ("s0", 0), ("s0", 1), ("s64", 0), ("s64", 1)]
    gcnt = [0]
    gmode = {"wide": True}
    WIDE = [0, 1, 2, 4, 5, 6, 7]
    NARROW = [0]
    PV_BANKS = [1, 2]
    pvc = [0]

    def gbank():
        pool = WIDE if gmode["wide"] else NARROW
        i = pool[gcnt[0] % len(pool)]
        gcnt[0] += 1
        return i

    def bkey(b):
        return BKEYS[b]

    TB_BANK = 3

    for i in range(x_in_tiles):
        P.op("sp", lambda e, i=i: e.dma_start(out=x_res[:, i, :], in_=x_d[i * TK:(i + 1) * TK, :]),
             writes=[("x", i)], dma=True)

    P.op("pool", lambda e: e.dma_start(out=psw[:, :], in_=psw_d), writes=["psw"], dma=True)
    P.op("pool", lambda e: e.memset(identf[:, :], 1.0), writes=["identf"])
    P.op("pool", lambda e: e.affine_select(out=identf[:, :], in_=identf[:, :], pattern=[[-1, 128]],
                                           compare_op=ALU.is_equal, fill=0.0, base=0, channel_multiplier=1),
         reads=["identf"], writes=["identf"])
    P.op("pool", lambda e: e.tensor_copy(out=ident[:, :], in_=identf[:, :]), reads=["identf"], writes=["ident"])
    P.op("pool", lambda e: e.memset(ones256[:, :], 1.0 / 256.0), writes=["ones256"])
    P.op("pool", lambda e: e.memset(mask3[:, :, :], 1.0), writes=["mask3"])
    P.op("pool", lambda e: e.affine_select(out=mask3[:, 0, :], in_=mask3[:, 0, :], pattern=[[-1, 128]],
                                           compare_op=ALU.is_ge, fill=0.0, base=0, channel_multiplier=1),
         reads=["mask3"], writes=["mask3"])
    P.op("pool", lambda e: e.affine_select(out=mask3[:, 2, :], in_=mask3[:, 2, :], pattern=[[1, 128]],
                                           compare_op=ALU.is_ge, fill=0.0, base=0, channel_multiplier=-1),
         reads=["mask3"], writes=["mask3"])

    wstate = {"n": 0}

    def load_unit(l, name):
        slot = wstate["n"] % 2
        wstate["n"] += 1
        off, nb = UNIT_OFF[name]
        src = L[l]["win"][:, off:off + nb * 128].rearrange("(c p) n -> p c n", p=128)
        P.op("pool", lambda e: e.dma_start(out=wring[slot][:, :, 0:nb * 128], in_=src),
             writes=[("w", slot)], dma=True)
        return slot

    def load_wo(l, pair):
        src = L[l]["wout"][pair * 512:(pair + 1) * 512, :].rearrange("(c p) n -> p c n", p=128)
        P.op("pool", lambda e: e.dma_start(out=wo[:, :, :], in_=src), writes=["wo"], dma=True)

    def hkeys(t0, n):
        return [("hT", t) for t in range(t0 // TK, (t0 + n) // TK)]

    def proj_fm(slot, blk, t0, n, bank):
        for c in range(8):
            P.op("pe", lambda e, c=c: e.matmul(gen[bank][:, 0:n], lhsT=wring[slot][:, c, blk * 128:(blk + 1) * 128],
                                               rhs=hT[:, c, t0:t0 + n], start=(c == 0), stop=(c == 7)),
                 reads=[("w", slot)] + hkeys(t0, n), writes=[bkey(bank)])

    def proj_tm(slot, col0, ncol, tile, bank):
        for c in range(8):
            P.op("pe", lambda e, c=c: e.matmul(gen[bank][:, 0:ncol], lhsT=hT[:, c, tile * TK:(tile + 1) * TK],
                                               rhs=wring[slot][:, c, col0:col0 + ncol], start=(c == 0), stop=(c == 7)),
                 reads=[("w", slot), ("hT", tile)], writes=[bkey(bank)])

    first_layer = layers[0]
    next_slot = {"slot": load_unit(first_layer, "A_key")}
    order = ["A_key", "A_q", "B_key", "B_q", "C_key", "C_z", "C_q", "D_key", "D_q"]

    def advance(l, cur):
        slot = next_slot["slot"]
        i = order.index(cur)
        if i + 1 < len(order):
            next_slot["slot"] = load_unit(l, order[i + 1])
        else:
            li = layers.index(l)
            if li + 1 < len(layers):
                next_slot["slot"] = load_unit(layers[li + 1], "A_key")
        return slot

    def run_pipeline(items, proj_fn, post_fn, post_pe_fn, nsb, warm_fn=None, nwarm=0):
        for st in proj_fn(0):
            st()
        steps = []
        pend = None
        pend_pe = []
        cur_bo = [None]

        def finish(it):
            if it["first"]:
                cur_bo[0] = PV_BANKS[pvc[0] % 2]
                pvc[0] += 1
            it["bo"] = cur_bo[0]
            it["PV"]()
            for _ in range(nwarm):
                warm_fn(it["bo"])
            if it["last"]:
                post_fn(it)
                it["_age"] = 0
                pend_pe.append(it)

        queue = []
        next_proj = [1]
        sb_done = {}
        cur_k = [0]
        for it in items:
            if it["k"] != cur_k[0]:
                cur_k[0] = it["k"]
                while steps:
                    steps.pop(0)()
            if next_proj[0] < nsb and it["k"] >= next_proj[0] - 1 and (next_proj[0] < 2 or sb_done.get(next_proj[0] - 2)):
                steps.extend(proj_fn(next_proj[0]))
                next_proj[0] += 1
            if steps and (it["k"] + 1 < nsb) and it["ti"] >= 1:
                steps.pop(0)()
            it["S"]()
            it["E"]()
            if it.get("filler") is not None:
                it["filler"]()
            while pend_pe and pend_pe[0]["_age"] >= PE_DEFER:
                pit = pend_pe.pop(0)
                post_pe_fn(pit)
                if pit["last_of_sb"]:
                    sb_done[pit["k"]] = True
            for pit in pend_pe:
                pit["_age"] += 1
            queue.append(it)
            if len(queue) > PIPE_DEPTH:
                finish(queue.pop(0))
        while queue:
            finish(queue.pop(0))
            for pit in list(pend_pe):
                post_pe_fn(pit)
            del pend_pe[:]

    def norm_begin(l, fresh_phase, stack=None, nxs=2):
        if fresh_phase:
            phase_begin()
        tag = "f" if l is None else str(l)
        ctx = dict(final=(l is None))
        WSx = WS if stack is None else (lambda name, shape, dt: stack.enter_context(nc.sbuf_tensor("sb_" + name, shape, dt)))
        ctx["xs"] = [WSx("xs%s_%d" % (tag, i), [128, D], BF16) for i in range(nxs)]
        ctx["bank"] = {}
        ctx["junk"] = WSx("junk%s" % tag, [128, D], BF16)
        ctx["g_bc"] = WSx("g_bc%s" % tag, [128, D], F32)
        ctx["kj"], ctx["kg"], ctx["kx"] = P.K("junk"), P.K("g_bc"), [P.K("xs", i) for i in range(nxs)]
        src = fg_d if l is None else L[l]["ng"]
        g_bc = ctx["g_bc"]
        P.op("act" if (l is not None and l == layers[0] and not fresh_phase) else "sp",
             lambda e: e.dma_start(out=g_bc[:, :], in_=src.broadcast_to([128, D])), writes=[ctx["kg"]], dma=True)
        return ctx

    def norm_batch(ctx, tl, stage="all"):
        xs, junk, g_bc = ctx["xs"], ctx["junk"], ctx["g_bc"]
        junkf = ctx["junk"]
        nx = len(xs)
        if stage in ("b", "c"):
            for i in tl:
                s_ = i % nx
                if stage == "b":
                    bk = gbank()
                    pv = gen[bk][:, :].bitcast(BF16)
                    ctx["bank"][i] = (bk, pv)
                    for c in range(8):
                        P.op("pe", lambda e, c=c, s_=s_, pv=pv: e.transpose(out=pv[:, c * 128:(c + 1) * 128],
                                                                            in_=xs[s_][:, c * 128:(c + 1) * 128],
                                                                            identity=ident[:, :]),
                             reads=[ctx["kx"][s_], "ident"], writes=[bkey(bk)])
                else:
                    bk, pv = ctx["bank"][i]
                    if ctx.get("copy_alt") and i % 2 == 1:
                        P.op("dve", lambda e, i=i, pv=pv: e.tensor_copy(out=hT[:, :, i * TK:(i + 1) * TK],
                                                                        in_=pv.rearrange("p (c t) -> p c t", c=8)),
                             reads=[bkey(bk)], writes=[("hT", i)])
                    else:
                        P.op("act", lambda e, i=i, pv=pv: e.activation(out=hT[:, :, i * TK:(i + 1) * TK],
                                                                       in_=pv.rearrange("p (c t) -> p c t", c=8), func=AF.Copy),
                             reads=[bkey(bk)], writes=[("hT", i)])
            return
        for i in tl:
            if SSQ_ON_DVE:
                P.op("dve", lambda e, i=i: e.scalar_tensor_tensor(out=junkf[:, :], in0=x_res[:, i, :], scalar=1.0,
                                                                   in1=x_res[:, i, :], op0=ALU.mult, op1=ALU.mult,
                                                                   accum_out=ssq[:, i:i + 1]),
                     reads=[("x", i)], writes=[ctx["kj"], ("ssq", i)])
            else:
                P.op("act", lambda e, i=i: e.activation(out=junk[:, :], in_=x_res[:, i, :], func=AF.Square,
                                                        accum_out=ssq[:, i:i + 1]),
                     reads=[("x", i)], writes=[ctx["kj"], ("ssq", i)])
        a, b = tl[0], tl[-1] + 1
        P.op("act", lambda e: e.activation(out=rstd[:, a:b], in_=ssq[:, a:b], func=AF.Sqrt, bias=EPS, scale=1.0 / D),
             reads=[("ssq", i) for i in tl], writes=[("rstd", i) for i in tl])
        P.op("dve", lambda e: e.reciprocal(out=rstd[:, a:b], in_=rstd[:, a:b]),
             reads=[("rstd", i) for i in tl], writes=[("rstd", i) for i in tl])
        for i in tl:
            if ctx["final"]:
                P.op("dve", lambda e, i=i: e.scalar_tensor_tensor(out=x_res[:, i, :], in0=x_res[:, i, :], scalar=rstd[:, i:i + 1],
                                                                  in1=g_bc[:, :], op0=ALU.mult, op1=ALU.mult),
                     reads=[("x", i), ("rstd", i), ctx["kg"]], writes=[("x", i)])
                P.op("sp", lambda e, i=i: e.dma_start(out=out_d[i * TK:(i + 1) * TK, :], in_=x_res[:, i, :]),
                     reads=[("x", i)], dma=True)
                continue
            s_ = i % nx
            P.op("dve", lambda e, i=i, s_=s_: e.scalar_tensor_tensor(out=xs[s_][:, :], in0=x_res[:, i, :],
                                                                     scalar=rstd[:, i:i + 1], in1=g_bc[:, :],
                                                                     op0=ALU.mult, op1=ALU.mult),
                 reads=[("x", i), ("rstd", i), ctx["kg"]], writes=[ctx["kx"][s_]])
            if stage == "a":
                continue
            bk = gbank()
            pv = gen[bk][:, :].bitcast(BF16)
            for c in range(8):
                P.op("pe", lambda e, c=c, s_=s_, pv=pv: e.transpose(out=pv[:, c * 128:(c + 1) * 128],
                                                                    in_=xs[s_][:, c * 128:(c + 1) * 128],
                                                                    identity=ident[:, :]),
                     reads=[ctx["kx"][s_], "ident"], writes=[bkey(bk)])
            P.op("act", lambda e, i=i, pv=pv: e.activation(out=hT[:, :, i * TK:(i + 1) * TK],
                                                           in_=pv.rearrange("p (c t) -> p c t", c=8), func=AF.Copy),
                 reads=[bkey(bk)], writes=[("hT", i)])

    def next_stats(ctx, i):
        junk = ctx["junk"]
        P.op("act", lambda e: e.activation(out=junk[:, :], in_=x_res[:, i, :], func=AF.Square, accum_out=ssq[:, i:i + 1]),
             reads=[("x", i)], writes=[ctx["kj"], ("ssq", i)])
        P.op("act", lambda e: e.activation(out=rstd[:, i:i + 1], in_=ssq[:, i:i + 1], func=AF.Sqrt, bias=EPS, scale=1.0 / D),
             reads=[("ssq", i)], writes=[("rstd", i)])

    def next_xs(ctx, i):
        xs, g_bc = ctx["xs"], ctx["g_bc"]
        s_ = i % len(xs)
        P.op("dve", lambda e: e.reciprocal(out=rstd[:, i:i + 1], in_=rstd[:, i:i + 1]), reads=[("rstd", i)], writes=[("rstd", i)])
        P.op("dve", lambda e: e.scalar_tensor_tensor(out=xs[s_][:, :], in0=x_res[:, i, :], scalar=rstd[:, i:i + 1],
                                                     in1=g_bc[:, :], op0=ALU.mult, op1=ALU.mult),
             reads=[("x", i), ("rstd", i), ctx["kg"]], writes=[ctx["kx"][s_]])

    def final_stats(ctx, i):
        junk = ctx["junk"]
        P.op("act", lambda e: e.activation(out=junk[:, :], in_=x_res[:, i, :], func=AF.Square, accum_out=ssq[:, i:i + 1]),
             reads=[("x", i)], writes=[ctx["kj"], ("ssq", i)])
        P.op("act", lambda e: e.activation(out=rstd[:, i:i + 1], in_=ssq[:, i:i + 1], func=AF.Sqrt, bias=EPS, scale=1.0 / D),
             reads=[("ssq", i)], writes=[("rstd", i)])

    def final_finish(ctx, i):
        g_bc = ctx["g_bc"]
        P.op("dve", lambda e: e.reciprocal(out=rstd[:, i:i + 1], in_=rstd[:, i:i + 1]), reads=[("rstd", i)], writes=[("rstd", i)])
        P.op("dve", lambda e: e.scalar_tensor_tensor(out=x_res[:, i, :], in0=x_res[:, i, :], scalar=rstd[:, i:i + 1],
                                                     in1=g_bc[:, :], op0=ALU.mult, op1=ALU.mult),
             reads=[("x", i), ("rstd", i), ctx["kg"]], writes=[("x", i)])
        P.op("sp", lambda e: e.dma_start(out=out_d[i * TK:(i + 1) * TK, :], in_=x_res[:, i, :]), reads=[("x", i)], dma=True)

    def do_layer(li, l):
        nk, nq = NK[l], NQ[l]
        W = L[l]
        ksbs = _sbs(nk)
        qsbs = _sbs(nq)
        phase_begin()
        TA = nk * TK + 2 * PADA
        hA = WS("hA%d" % l, [128, 2, TA], BF16)
        diagA = WS("diagA%d" % l, [128, 2, 31, 128], BF16)
        caw = WS("caw%d" % l, [128, 2, 31], F32)
        vec = WS("vecA%d" % l, [128, 6], F32)
        var_sb = WS("var%d" % l, [128, 512], F32)
        sig = var_sb
        P.op("sp", lambda e: e.dma_start(out=caw[:, :, :], in_=W["caw"]), writes=[P.K("caw")], dma=True)
        P.op("sp", lambda e: e.dma_start(out=vec[:, 0:2], in_=W["cab"]), writes=[P.K("vec")], dma=True)
        P.op("sp", lambda e: e.dma_start(out=vec[:, 2:4], in_=W["lng"]), writes=[P.K("vec")], dma=True)
        P.op("sp", lambda e: e.dma_start(out=vec[:, 4:6], in_=W["lnb"]), writes=[P.K("vec")], dma=True)
        P.op("pool", lambda e: e.memset(hA[:, :, 0:PADA], 0.0), writes=[P.K("hA", 0), P.K("hA", 1)])
        P.op("pool", lambda e: e.memset(hA[:, :, PADA + nk * TK:TA], 0.0), writes=[P.K("hA", 0), P.K("hA", 1)])
        slot = advance(l, "A_key")
        diag_todo = [(c, j) for c in range(2) for j in range(31)]

        def build_diag(nmax):
            for _ in range(nmax):
                if not diag_todo:
                    return
                c, j = diag_todo.pop(0)
                P.op("dve", lambda e, c=c, j=j: e.tensor_scalar(out=diagA[:, c, j, :], in0=identf[:, :],
                                                                 scalar1=caw[:, c, j:j + 1], scalar2=None, op0=ALU.mult),
                     reads=["identf", P.K("caw")], writes=[P.K("diagA", c, j)])

        def a_key(t0, n):
            for c in range(2):
                bu, bv = gbank(), gbank()
                proj_fm(slot, c, t0, n, bu)
                proj_fm(slot, 2 + c, t0, n, bv)
                P.op("act", lambda e, bv=bv, n=n: e.activation(out=sig[:, 0:n], in_=gen[bv][:, 0:n], func=AF.Sigmoid),
                     reads=[bkey(bv)], writes=[P.K("var")])
                P.op("dve", lambda e, c=c, bu=bu, t0=t0, n=n: e.tensor_tensor(out=hA[:, c, PADA + t0:PADA + t0 + n],
                                                                              in0=gen[bu][:, 0:n], in1=sig[:, 0:n], op=ALU.mult),
                     reads=[bkey(bu), P.K("var")], writes=[P.K("hA", c)])

        if li == 0:
            child = ExitStack()
            nctx = norm_begin(l, False, child, nxs=4)
            nctx["copy_alt"] = True
            tls = [list(range(t0 // TK, (t0 + n) // TK)) for (t0, n) in ksbs]
            for t_ in tls[0]:
                norm_batch(nctx, [t_])
            for k_, (t0, n) in enumerate(ksbs):
                if k_ + 1 < len(ksbs):
                    for t_ in tls[k_ + 1]:
                        next_stats(nctx, t_)
                        next_xs(nctx, t_)
                a_key(t0, n)
                if k_ + 1 < len(ksbs):
                    norm_batch(nctx, tls[k_ + 1], "b")
                    norm_batch(nctx, tls[k_ + 1], "c")
            child.close()
            P.soft_fence()
        else:
            for (t0, n) in _sbs(min(nk, nq + 1)):
                a_key(t0, n)
                build_diag(16)
        build_diag(99)
        slot = advance(l, "A_q")
        szA = [WS("szA%d_%d" % (l, c), [128, 512], BF16) for c in range(2)]
        cf = [WS("cf%d_%d" % (l, c), [128, 512], F32) for c in range(2)]
        cbf = [WS("cbf%d_%d" % (l, c), [128, 512], BF16) for c in range(2)]
        csq = [WS("csq%d_%d" % (l, c), [128, 512], BF16) for c in range(2)]
        for (t0, n) in qsbs:
            for c in range(2):
                bz = gbank()
                proj_fm(slot, c, t0, n, bz)
                P.op("act", lambda e, c=c, bz=bz, n=n: e.activation(out=szA[c][:, 0:n], in_=gen[bz][:, 0:n], func=AF.Silu),
                     reads=[bkey(bz)], writes=[P.K("szA", c)])
            for c in range(2):
                bc = gbank()
                for j in range(31):
                    o = PADA + t0 + j - 15
                    P.op("pe", lambda e, c=c, j=j, o=o, bc=bc, n=n: e.matmul(gen[bc][:, 0:n], lhsT=diagA[:, c, j, :],
                                                                             rhs=hA[:, c, o:o + n], start=(j == 0), stop=(j == 30)),
                         reads=[P.K("diagA", c, j), P.K("hA", c)], writes=[bkey(bc)])
                P.op("act", lambda e, c=c, bc=bc, n=n: e.activation(out=cf[c][:, 0:n], in_=gen[bc][:, 0:n], func=AF.Identity,
                                                                    bias=vec[:, c:c + 1]),
                     reads=[bkey(bc), P.K("vec")], writes=[P.K("cf", c)])
                P.op("act", lambda e, c=c, bc=bc, n=n: e.activation(out=cbf[c][:, 0:n], in_=gen[bc][:, 0:n], func=AF.Identity,
                                                                    bias=vec[:, c:c + 1]),
                     reads=[bkey(bc), P.K("vec")], writes=[P.K("cbf", c)])
                P.op("act", lambda e, c=c, bc=bc, n=n: e.activation(out=csq[c][:, 0:n], in_=gen[bc][:, 0:n], func=AF.Square,
                                                                    bias=vec[:, c:c + 1]),
                     reads=[bkey(bc), P.K("vec")], writes=[P.K("csq", c)])
            bm, be = gbank(), gbank()
            for c in range(2):
                P.op("pe", lambda e, c=c, bm=bm, n=n: e.matmul(gen[bm][:, 0:n], lhsT=ones256[:, :], rhs=cbf[c][:, 0:n],
                                                               start=(c == 0), stop=(c == 1)),
                     reads=["ones256", P.K("cbf", c)], writes=[bkey(bm)])
            for c in range(2):
                P.op("pe", lambda e, c=c, be=be, n=n: e.matmul(gen[be][:, 0:n], lhsT=ones256[:, :], rhs=csq[c][:, 0:n],
                                                               start=(c == 0), stop=(c == 1)),
                     reads=["ones256", P.K("csq", c)], writes=[bkey(be)])
            P.op("act", lambda e, bm=bm, n=n: e.activation(out=var_sb[:, 0:n], in_=gen[bm][:, 0:n], func=AF.Square),
                 reads=[bkey(bm)], writes=[P.K("var")])
            P.op("dve", lambda e, be=be, n=n: e.tensor_tensor(out=var_sb[:, 0:n], in0=gen[be][:, 0:n], in1=var_sb[:, 0:n],
                                                              op=ALU.subtract),
                 reads=[bkey(be), P.K("var")], writes=[P.K("var")])
            P.op("dve", lambda e, n=n: e.tensor_scalar(out=var_sb[:, 0:n], in0=var_sb[:, 0:n], scalar1=0.0, scalar2=None,
                                                       op0=ALU.max),
                 reads=[P.K("var")], writes=[P.K("var")])
            P.op("act", lambda e, n=n: e.activation(out=var_sb[:, 0:n], in_=var_sb[:, 0:n], func=AF.Sqrt, bias=EPS, scale=1.0),
                 reads=[P.K("var")], writes=[P.K("var")])
            P.op("dve", lambda e, n=n: e.reciprocal(out=var_sb[:, 0:n], in_=var_sb[:, 0:n]),
                 reads=[P.K("var")], writes=[P.K("var")])
            for c in range(2):
                P.op("dve", lambda e, c=c, n=n, bm=bm: e.tensor_tensor(out=cf[c][:, 0:n], in0=cf[c][:, 0:n], in1=gen[bm][:, 0:n],
                                                                       op=ALU.subtract),
                     reads=[P.K("cf", c), bkey(bm)], writes=[P.K("cf", c)])
                P.op("dve", lambda e, c=c, n=n: e.tensor_tensor(out=cf[c][:, 0:n], in0=cf[c][:, 0:n], in1=var_sb[:, 0:n],
                                                                op=ALU.mult),
                     reads=[P.K("cf", c), P.K("var")], writes=[P.K("cf", c)])
                P.op("act", lambda e, c=c, t0=t0, n=n: e.activation(out=yT[:, c, t0:t0 + n], in_=cf[c][:, 0:n], func=AF.Silu,
                                                                    scale=vec[:, 2 + c:3 + c], bias=vec[:, 4 + c:5 + c]),
                     reads=[P.K("cf", c), P.K("vec")], writes=[("yT", c)])
                P.op("dve", lambda e, c=c, t0=t0, n=n: e.tensor_tensor(out=yT[:, c, t0:t0 + n], in0=yT[:, c, t0:t0 + n],
                                                                        in1=szA[c][:, 0:n], op=ALU.mult),
                     reads=[("yT", c), P.K("szA", c)], writes=[("yT", c)])

        phase_begin()
        TB = nk * TK + 2 * PADB
        cx = WS("cx%d" % l, [128, 2, TB], BF16)
        diagB = WS("diagB%d" % l, [128, 2, 3, 128], BF16)
        cbw = WS("cbw%d" % l, [128, 2, 3], F32)
        csb = WS("csb%d" % l, [128, 512], F32)
        P.op("sp", lambda e: e.dma_start(out=cbw[:, :, :], in_=W["cbw"]), writes=[P.K("cbw")], dma=True)
        P.op("pool", lambda e: e.memset(cx[:, :, 0:PADB], 0.0), writes=[P.K("cx", 0), P.K("cx", 1)])
        P.op("pool", lambda e: e.memset(cx[:, :, PADB + nk * TK:TB], 0.0), writes=[P.K("cx", 0), P.K("cx", 1)])
        for c in range(2):
            for j in range(3):
                P.op("dve", lambda e, c=c, j=j: e.tensor_scalar(out=diagB[:, c, j, :], in0=identf[:, :],
                                                                 scalar1=cbw[:, c, j:j + 1], scalar2=None, op0=ALU.mult),
                     reads=["identf", P.K("cbw")], writes=[P.K("diagB", c)])
        slot = advance(l, "B_key")
        for (t0, n) in _sbs(min(nk, nq + 1)):
            for c in range(2):
                b1, b2 = gbank(), gbank()
                proj_fm(slot, c, t0, n, b1)
                proj_fm(slot, 2 + c, t0, n, b2)
                P.op("act", lambda e, b1=b1, n=n: e.activation(out=csb[:, 0:n], in_=gen[b1][:, 0:n], func=AF.Copy),
                     reads=[bkey(b1)], writes=[P.K("csb")])
                P.op("dve", lambda e, c=c, b2=b2, t0=t0, n=n: e.tensor_tensor(out=cx[:, c, PADB + t0:PADB + t0 + n],
                                                                              in0=gen[b2][:, 0:n], in1=csb[:, 0:n], op=ALU.mult),
                     reads=[bkey(b2), P.K("csb")], writes=[P.K("cx", c)])
        slot = advance(l, "B_q")
        load_wo(l, 0)
        bsb = [WS("bsb%d_%d" % (l, c), [128, 512], F32) for c in range(2)]
        szB = [WS("szB%d_%d" % (l, c), [128, 512], F32) for c in range(2)]
        for (t0, n) in qsbs:
            for c in range(2):
                b1, b2 = gbank(), gbank()
                proj_fm(slot, c, t0, n, b1)
                proj_fm(slot, 2 + c, t0, n, b2)
                P.op("act", lambda e, c=c, b1=b1, n=n: e.activation(out=bsb[c][:, 0:n], in_=gen[b1][:, 0:n], func=AF.Copy),
                     reads=[bkey(b1)], writes=[P.K("bsb", c)])
                P.op("act", lambda e, c=c, b2=b2, n=n: e.activation(out=szB[c][:, 0:n], in_=gen[b2][:, 0:n], func=AF.Silu),
                     reads=[bkey(b2)], writes=[P.K("szB", c)])
                P.op("dve", lambda e, c=c, n=n: e.tensor_tensor(out=bsb[c][:, 0:n], in0=bsb[c][:, 0:n], in1=szB[c][:, 0:n],
                                                                 op=ALU.mult),
                     reads=[P.K("bsb", c), P.K("szB", c)], writes=[P.K("bsb", c)])
                bc = gbank()
                for j in range(3):
                    o = PADB + t0 + j - 1
                    P.op("pe", lambda e, c=c, j=j, o=o, bc=bc, n=n: e.matmul(gen[bc][:, 0:n], lhsT=diagB[:, c, j, :],
                                                                             rhs=cx[:, c, o:o + n], start=(j == 0), stop=(j == 2)),
                         reads=[P.K("diagB", c), P.K("cx", c)], writes=[bkey(bc)])
                P.op("dve", lambda e, c=c, bc=bc, t0=t0, n=n: e.tensor_tensor(out=yT[:, 2 + c, t0:t0 + n], in0=gen[bc][:, 0:n],
                                                                              in1=bsb[c][:, 0:n], op=ALU.mult),
                     reads=[bkey(bc), P.K("bsb", c)], writes=[("yT", 2 + c)])

        def outproj_tile(i, banks=None, halves=(0, 1)):
            for half in halves:
                bk = gbank() if banks is None else banks[half]
                for c in range(4):
                    P.op("pe", lambda e, c=c, half=half, bk=bk: e.matmul(
                        gen[bk][:, :], lhsT=yT[:, c, i * TK:(i + 1) * TK], rhs=wo[:, c, half * 512:(half + 1) * 512],
                        start=(c == 0), stop=(c == 3)),
                        reads=[("yT", c), "wo"], writes=[bkey(bk)])
                P.op("dve", lambda e, half=half, bk=bk: e.tensor_tensor(
                    out=x_res[:, i, half * 512:(half + 1) * 512], in0=gen[bk][:, :],
                    in1=x_res[:, i, half * 512:(half + 1) * 512], op=ALU.add),
                    reads=[bkey(bk), ("x", i)], writes=[("x", i)])

        def outproj(pair, after_sb=None, fine_last=False, per_tile=False):
            for k_, (t0_, n_) in enumerate(qsbs):
                tl_ = list(range(t0_ // TK, (t0_ + n_) // TK))
                if (per_tile or (fine_last and k_ == len(qsbs) - 1)) and after_sb is not None:
                    for i in tl_:
                        outproj_tile(i)
                        after_sb([i])
                    continue
                for i in tl_:
                    outproj_tile(i)
                if after_sb is not None:
                    after_sb(tl_)


        phase_begin()
        kr = WS("kr%d" % l, [128, nk * TK], BF16)
        Vc = WS("Vc%d" % l, [128, nk, 2, 65], BF16)
        rcs = [WS("rcs%d_%d" % (l, i), [128, 2, 512], F32) for i in range(2)]
        t1 = WS("t1_%d" % l, [128, 512], F32)
        t2 = WS("t2_%d" % l, [128, 512], F32)
        szC = WS("szC%d" % l, [128, 2, nq * TK], BF16)
        esink = WS("esink%d" % l, [128, 4], F32)
        P.op("sp", lambda e: e.dma_start(out=esink[:, :], in_=W["sink"].broadcast_to([128, 4])), writes=[P.K("esink")], dma=True)
        P.op("act", lambda e: e.activation(out=esink[:, :], in_=esink[:, :], func=AF.Exp), reads=[P.K("esink")], writes=[P.K("esink")])
        P.op("pool", lambda e: e.memset(Vc[:, :, :, 64:65], 1.0), writes=[P.K("Vc", i) for i in range(nk)])
        rcnt = [0]

        def load_rope(t0, n):
            s = rcnt[0] % 2
            rcnt[0] += 1
            P.op("sp", lambda e: e.dma_start(out=rcs[s][:, 0, 0:n], in_=rc_d[:, t0:t0 + n]), writes=[P.K("rcs", s)], dma=True)
            P.op("sp", lambda e: e.dma_start(out=rcs[s][:, 1, 0:n], in_=rs_d[:, t0:t0 + n]), writes=[P.K("rcs", s)], dma=True)
            return s

        xb = WS("xb%d" % l, [128, 512], BF16)

        def swap_mm(bx, bsw, n):
            P.op("act", lambda e: e.activation(out=xb[:, 0:n], in_=gen[bx][:, 0:n], func=AF.Copy),
                 reads=[bkey(bx)], writes=[P.K("xb")])
            P.op("pe", lambda e: e.matmul(gen[bsw][:, 0:n], lhsT=psw[:, :], rhs=xb[:, 0:n], start=True, stop=True),
                 reads=["psw", P.K("xb")], writes=[bkey(bsw)])

        def rope(bx, bsw, s, n, out_ap, out_keys):
            P.op("dve", lambda e: e.tensor_tensor(out=t1[:, 0:n], in0=xb[:, 0:n], in1=rcs[s][:, 0, 0:n], op=ALU.mult),
                 reads=[P.K("xb"), P.K("rcs", s)], writes=[P.K("t1")])
            P.op("dve", lambda e: e.tensor_tensor(out=t2[:, 0:n], in0=gen[bsw][:, 0:n], in1=rcs[s][:, 1, 0:n], op=ALU.mult),
                 reads=[bkey(bsw), P.K("rcs", s)], writes=[P.K("t2")])
            P.op("dve", lambda e: e.tensor_tensor(out=out_ap, in0=t1[:, 0:n], in1=t2[:, 0:n], op=ALU.add),
                 reads=[P.K("t1"), P.K("t2")], writes=out_keys)

        slot = advance(l, "C_key")
        for (t0, n) in _sbs(min(nk, nq + 1)):
            s = load_rope(t0, n)
            bk_, bs_ = gbank(), gbank()
            proj_fm(slot, 0, t0, n, bk_)
            P.op("act", lambda e, bk_=bk_, n=n: e.activation(out=xb[:, 0:n], in_=gen[bk_][:, 0:n], func=AF.Copy),
                 reads=[bkey(bk_)], writes=[P.K("xb")])
            for i in range(t0 // TK, (t0 + n) // TK):
                bv = gbank()
                proj_tm(slot, 128, 128, i, bv)
                P.op("act", lambda e, i=i, bv=bv: e.activation(out=Vc[:, i, :, 0:64],
                                                               in_=gen[bv][:, 0:128].rearrange("p (h d) -> p h d", h=2),
                                                               func=AF.Copy),
                     reads=[bkey(bv)], writes=[P.K("Vc", i)])
            P.op("pe", lambda e, bs_=bs_, n=n: e.matmul(gen[bs_][:, 0:n], lhsT=psw[:, :], rhs=xb[:, 0:n], start=True, stop=True),
                 reads=["psw", P.K("xb")], writes=[bkey(bs_)])
            rope(bk_, bs_, s, n, kr[:, t0:t0 + n], [P.K("kr", t) for t in range(t0 // TK, (t0 + n) // TK)])
        slot = advance(l, "C_z")
        for (t0, n) in qsbs:
            for c in range(2):
                bz = gbank()
                proj_fm(slot, c, t0, n, bz)
                P.op("act", lambda e, c=c, bz=bz, t0=t0, n=n: e.activation(out=szC[:, c, t0:t0 + n], in_=gen[bz][:, 0:n],
                                                                           func=AF.Silu),
                     reads=[bkey(bz)], writes=[P.K("szC", c)])
        slot = advance(l, "C_q")
        gmode["wide"] = False
        NARROW[:] = [0, 7]
        qr = [WS("qr%d_%d" % (l, i), [128, 2, 512], BF16) for i in range(2)]
        PT = [WS("PT%d_%d" % (l, i), [128, 384], BF16) for i in range(3)]
        ytok = [WS("ytokC%d_%d" % (l, i), [128, 256], BF16) for i in range(2)]
        den = WS("denC%d" % l, [128, 4, 1], F32)
        tbC = gen[TB_BANK].bitcast(BF16)

        def c_proj(k):
            t0, n = qsbs[k]
            st = []
            hold = {}

            def s_rope():
                hold["s"] = load_rope(t0, n)
            st.append(s_rope)
            for c in range(2):
                def s_q(c=c):
                    s = hold["s"]
                    bq = gbank()
                    hold["bq", c] = bq
                    proj_fm(slot, c, t0, n, bq)
                    P.op("act", lambda e: e.activation(out=xb[:, 0:n], in_=gen[bq][:, 0:n], func=AF.Copy),
                         reads=[bkey(bq)], writes=[P.K("xb")])
                    P.op("dve", lambda e: e.tensor_tensor(out=t1[:, 0:n], in0=xb[:, 0:n], in1=rcs[s][:, 0, 0:n], op=ALU.mult),
                         reads=[P.K("xb"), P.K("rcs", s)], writes=[P.K("t1")])

                def s_sw(c=c):
                    s = hold["s"]
                    bs_ = gbank()
                    P.op("pe", lambda e: e.matmul(gen[bs_][:, 0:n], lhsT=psw[:, :], rhs=xb[:, 0:n], start=True, stop=True),
                         reads=["psw", P.K("xb")], writes=[bkey(bs_)])
                    P.op("dve", lambda e: e.tensor_tensor(out=t2[:, 0:n], in0=gen[bs_][:, 0:n], in1=rcs[s][:, 1, 0:n], op=ALU.mult),
                         reads=[bkey(bs_), P.K("rcs", s)], writes=[P.K("t2")])
                    P.op("dve", lambda e: e.tensor_tensor(out=qr[k % 2][:, c, 0:n], in0=t1[:, 0:n], in1=t2[:, 0:n], op=ALU.add),
                         reads=[P.K("t1"), P.K("t2")], writes=[P.K("qr", k % 2, c)])
                st.append(s_q)
                st.append(s_sw)
            return st

        items = []
        cnt = [0]
        for k, (t0, n) in enumerate(qsbs):
            ntile = n // TK
            for ti in range(ntile):
                i = t0 // TK + ti
                kbs = [kb for kb in (i - 1, i, i + 1) if 0 <= kb < nk]
                mlo = kbs[0] - (i - 1)
                nkb = len(kbs)
                for hidx, (c, base) in enumerate([(0, 0), (0, 64), (1, 0), (1, 64)]):
                    sc = s0 if base == 0 else s64
                    sk = "s0" if base == 0 else "s64"
                    half = 0
                    scv = sc[:, 0:512]
                    p = cnt[0] % 3
                    cnt[0] += 1
                    it = {}

                    def S(kbs=kbs, base=base, c=c, ti=ti, scv=scv, sk=sk, half=half, k=k):
                        for jj, kb in enumerate(kbs):
                            P.op("pe", lambda e, jj=jj, kb=kb: e.matmul(
                                scv[:, jj * 128:(jj + 1) * 128], lhsT=kr[base:base + 64, kb * TK:(kb + 1) * TK],
                                rhs=qr[k % 2][base:base + 64, c, ti * TK:(ti + 1) * TK], start=True, stop=True),
                                reads=[P.K("kr", kb), P.K("qr", k % 2, c)], writes=[(sk, half)])

                    def E(p=p, scv=scv, nkb=nkb, mlo=mlo, sk=sk, half=half, kbs=kbs, i=i):
                        P.op("act", lambda e: e.activation(out=PT[p][:, 0:nkb * 128], in_=scv[:, 0:nkb * 128],
                                                           func=AF.Exp, scale=0.125),
                             reads=[(sk, half)], writes=[P.K("PT", p)])
                        if SWA_MASK_ON_POOL:
                            for jj, kb in enumerate(kbs):
                                if kb == i:
                                    continue
                                pat, cm = ([[-1, 128]], 1) if kb == i - 1 else ([[1, 128]], -1)
                                P.op("pool", lambda e, jj=jj, pat=pat, cm=cm: e.affine_select(
                                    out=PT[p][:, jj * 128:(jj + 1) * 128], in_=PT[p][:, jj * 128:(jj + 1) * 128], pattern=pat,
                                    compare_op=ALU.is_ge, fill=0.0, base=0, channel_multiplier=cm),
                                    reads=[P.K("PT", p)], writes=[P.K("PT", p)])
                        else:
                            P.op("dve", lambda e: e.tensor_tensor(
                                out=PT[p][:, 0:nkb * 128].rearrange("p (j q) -> p j q", q=128),
                                in0=PT[p][:, 0:nkb * 128].rearrange("p (j q) -> p j q", q=128),
                                in1=mask3[:, mlo:mlo + nkb, :], op=ALU.mult),
                                reads=[P.K("PT", p), "mask3"], writes=[P.K("PT", p)])

                    def PV(kbs=kbs, p=p, hidx=hidx, base=base, nkb=nkb, it=it):
                        bo = it["bo"]
                        kvh = base // 64
                        for jj, kb in enumerate(kbs):
                            P.op("pe", lambda e, jj=jj, kb=kb: e.matmul(
                                gen[bo][:, hidx * 65:(hidx + 1) * 65], lhsT=PT[p][:, jj * 128:(jj + 1) * 128],
                                rhs=Vc[:, kb, kvh, :], start=(jj == 0), stop=(jj == nkb - 1)),
                                reads=[P.K("PT", p), P.K("Vc", kb)], writes=[bkey(bo)])

                    it.update(S=S, E=E, PV=PV, first=(hidx == 0), last=(hidx == 3), ti=ti, k=k, t0=t0, n=n,
                              proj_trigger=(hidx == 0 and ti == 1), last_of_sb=(hidx == 3 and ti == ntile - 1))
                    if hidx == 1:
                        it["filler"] = (lambda i=i: outproj_tile(i, (5, 5), halves=(0,)))
                    if hidx == 3:
                        it["filler"] = (lambda i=i: outproj_tile(i, (5, 5), halves=(1,)))
                    items.append(it)

        def c_post(it):
            bo = it["bo"]
            ob3 = gen[bo][:, 0:260].rearrange("p (h e) -> p h e", e=65)
            y = it["ti"] % 2
            P.op("dve", lambda e: e.tensor_tensor(out=den[:, :, :], in0=ob3[:, :, 64:65],
                                                  in1=esink[:, :].rearrange("p (h o) -> p h o", o=1), op=ALU.add),
                 reads=[bkey(bo), P.K("esink")], writes=[P.K("den")])
            P.op("dve", lambda e: e.reciprocal(out=den[:, :, :], in_=den[:, :, :]), reads=[P.K("den")], writes=[P.K("den")])
            P.op("dve", lambda e: e.tensor_tensor(out=ytok[y][:, :].rearrange("p (h d) -> p h d", d=64),
                                                  in0=ob3[:, :, 0:64], in1=den[:, :, :].broadcast_to([128, 4, 64]),
                                                  op=ALU.mult),
                 reads=[bkey(bo), P.K("den")], writes=[P.K("ytok", y)])

        def c_post_pe(it):
            y = it["ti"] % 2
            ti = it["ti"]
            for c in range(2):
                P.op("pe", lambda e, c=c: e.transpose(
                    out=tbC[:, c * 512 + ti * 128:c * 512 + (ti + 1) * 128], in_=ytok[y][:, c * 128:(c + 1) * 128],
                    identity=ident[:, :]),
                    reads=[P.K("ytok", y), "ident"], writes=[bkey(TB_BANK)])
            if it["last_of_sb"]:
                t0, n = it["t0"], it["n"]
                for c in range(2):
                    P.op("dve", lambda e, c=c: e.tensor_tensor(out=yT[:, c, t0:t0 + n], in0=tbC[:, c * 512:c * 512 + n],
                                                               in1=szC[:, c, t0:t0 + n], op=ALU.mult),
                         reads=[bkey(TB_BANK), P.K("szC", c)], writes=[("yT", c)])

        def c_warm(bo):
            P.op("pe", lambda e: e.matmul(gen[bo][:, 384:512], lhsT=ident[:, :], rhs=ident[:, :], start=True, stop=True),
                 writes=[bkey(bo)])

        run_pipeline(items, c_proj, c_post, c_post_pe, len(qsbs), c_warm, NWARM_C)
        gmode["wide"] = True

        phase_begin()
        kD = WS("kD%d" % l, [128, 2, nk * TK], BF16)
        Vd = WS("Vd%d" % l, [128, nk, 4, 65], BF16)
        etab = WS("etab%d" % l, [128, 5, 4, 128], BF16)
        e01 = x_res[:, 18:20, :].bitcast(BF16).rearrange("p t (j h q) -> p (t j) h q", h=4, q=128)
        P.op("pool", lambda e: e.memset(Vd[:, :, :, 64:65], 1.0), writes=[P.K("Vd", i) for i in range(nk)])
        P.op("pool", lambda e: e.memset(x_res[:, 18:20, :], 0.0), writes=[("x", 18), ("x", 19)] + [("e01", j) for j in range(8)])

        def etab_of(cls):
            if cls == 2:
                return etab, (lambda j: P.K("etab", j)), 0
            return e01, (lambda j: ("e01", j)), 4 * cls

        eprep = []
        for cls in (0, 1, 2):
            dst, kf, joff = etab_of(cls)
            for j in range(NA_CLASS_CH[cls]):
                gi = NA_CLASS_OFF[cls] + j
                jj = joff + j
                rd = []
                P.op("pool", lambda e, gi=gi, dst=dst, jj=jj: e.dma_start(out=dst[:, jj, :, :], in_=W["nab"][:, gi, :, :]),
                     reads=rd, writes=[kf(jj)], dma=True)
                eprep.append((dst, [kf(jj)], jj))

        def emit_eprep(nmax):
            for _ in range(nmax):
                if not eprep:
                    return
                dst, dkeys, jj = eprep.pop(0)
                P.op("act", lambda e, dst=dst, jj=jj: e.activation(out=dst[:, jj, :, :], in_=dst[:, jj, :, :], func=AF.Exp),
                     reads=dkeys, writes=dkeys)
        slot = advance(l, "D_key")
        for (t0, n) in ksbs:
            for c in range(2):
                bk_ = gbank()
                proj_fm(slot, c, t0, n, bk_)
                P.op("act", lambda e, c=c, bk_=bk_, t0=t0, n=n: e.activation(out=kD[:, c, t0:t0 + n], in_=gen[bk_][:, 0:n],
                                                                            func=AF.Copy),
                     reads=[bkey(bk_)], writes=[P.K("kD", c, t) for t in range(t0 // TK, (t0 + n) // TK)])
            for i in range(t0 // TK, (t0 + n) // TK):
                bv = gbank()
                proj_tm(slot, 256, 256, i, bv)
                P.op("act", lambda e, i=i, bv=bv: e.activation(out=Vd[:, i, :, 0:64],
                                                               in_=gen[bv][:, 0:256].rearrange("p (h d) -> p h d", h=4),
                                                               func=AF.Copy),
                     reads=[bkey(bv)], writes=[P.K("Vd", i)])
            emit_eprep(3)
        emit_eprep(99)
        slot = advance(l, "D_q")
        load_wo(l, 1)
        for (t0, n) in qsbs:
            hk = hkeys(t0, n)
            bks = [gbank() for _ in range(4)]
            for b_, bk_ in enumerate(bks):
                proj_fm(slot, b_, t0, n, bk_)
            for c in range(2):
                P.op("act", lambda e, c=c, bk_=bks[c], t0=t0, n=n: e.activation(out=hT[:, c, t0:t0 + n], in_=gen[bk_][:, 0:n],
                                                                               func=AF.Copy),
                     reads=[bkey(bks[c])], writes=hk)
            for c in range(2):
                P.op("act", lambda e, c=c, bk_=bks[2 + c], t0=t0, n=n: e.activation(out=hT[:, 2 + c, t0:t0 + n],
                                                                                   in_=gen[bk_][:, 0:n], func=AF.Silu),
                     reads=[bkey(bks[2 + c])], writes=hk)
        gmode["wide"] = False
        NARROW[:] = [0]
        PTd = [WS("PTd%d_%d" % (l, i), [128, 640], BF16) for i in range(3)]
        ytokD = [WS("ytokD%d_%d" % (l, i), [128, 256], BF16) for i in range(2)]
        denD = WS("denD%d" % l, [128, 4, 1], F32)
        tbD = gen[TB_BANK].bitcast(BF16)

        def d_proj(k):
            return []

        items = []
        cur_cls = [-1]
        dcnt = [0]
        for k, (t0, n) in enumerate(qsbs):
            ntile = n // TK
            for ti in range(ntile):
                i = t0 // TK + ti
                cls = min(i, 2)
                lo = max(0, i - 2)
                hi = min(nk - 1, max(i + 2, 3))
                cl = list(range(lo, hi + 1))
                j0 = lo - (i - 2) if cls == 2 else lo
                ncl = len(cl)
                for h in range(4):
                    c, base = h // 2, (h % 2) * 64
                    sc = s0 if base == 0 else s64
                    sk = "s0" if base == 0 else "s64"
                    p = dcnt[0] % 3
                    dcnt[0] += 1
                    it = {}
                    need_cls = None
                    if h == 0 and cls != cur_cls[0]:
                        need_cls = cls
                        cur_cls[0] = cls

                    def S(cl=cl, base=base, c=c, ti=ti, sc=sc, sk=sk, k=k, i=i):
                        for jj, kc in enumerate(cl):
                            P.op("pe", lambda e, jj=jj, kc=kc: e.matmul(
                                sc[:, jj * 128:(jj + 1) * 128], lhsT=kD[base:base + 64, c, kc * TK:(kc + 1) * TK],
                                rhs=hT[base:base + 64, c, i * TK:(i + 1) * TK], start=True, stop=True),
                                reads=[P.K("kD", c, kc), ("hT", i)], writes=[(sk, 0), (sk, 1)])

                    def E(p=p, sc=sc, ncl=ncl, j0=j0, h=h, sk=sk, cls=cls):
                        etb, kf, joff = etab_of(cls)
                        ekeys = [kf(joff + j0 + jx) for jx in range(ncl)]
                        P.op("act", lambda e: e.activation(out=PTd[p][:, 0:ncl * 128], in_=sc[:, 0:ncl * 128],
                                                           func=AF.Exp, scale=0.125),
                             reads=[(sk, 0), (sk, 1)], writes=[P.K("PTd", p)])
                        P.op("dve", lambda e: e.tensor_tensor(
                            out=PTd[p][:, 0:ncl * 128].rearrange("p (j q) -> p j q", q=128),
                            in0=PTd[p][:, 0:ncl * 128].rearrange("p (j q) -> p j q", q=128),
                            in1=etb[:, joff + j0:joff + j0 + ncl, h, :], op=ALU.mult),
                            reads=[P.K("PTd", p)] + ekeys, writes=[P.K("PTd", p)])

                    def PV(cl=cl, p=p, h=h, ncl=ncl, it=it):
                        bo = it["bo"]
                        for jj, kc in enumerate(cl):
                            P.op("pe", lambda e, jj=jj, kc=kc: e.matmul(
                                gen[bo][:, h * 65:(h + 1) * 65], lhsT=PTd[p][:, jj * 128:(jj + 1) * 128],
                                rhs=Vd[:, kc, h, :], start=(jj == 0), stop=(jj == ncl - 1)),
                                reads=[P.K("PTd", p), P.K("Vd", kc)], writes=[bkey(bo)])

                    it.update(S=S, E=E, PV=PV, first=(h == 0), last=(h == 3), ti=ti, k=k, t0=t0, n=n,
                              proj_trigger=(h == 0 and ti == 1), last_of_sb=(h == 3 and ti == ntile - 1))
                    items.append(it)

        def d_post(it):
            bo = it["bo"]
            ob3 = gen[bo][:, 0:260].rearrange("p (h e) -> p h e", e=65)
            y = it["ti"] % 2
            P.op("dve", lambda e: e.reciprocal(out=denD[:, :, :], in_=ob3[:, :, 64:65]),
                 reads=[bkey(bo)], writes=[P.K("denD")])
            P.op("dve", lambda e: e.tensor_tensor(out=ytokD[y][:, :].rearrange("p (h d) -> p h d", d=64),
                                                  in0=ob3[:, :, 0:64], in1=denD[:, :, :].broadcast_to([128, 4, 64]),
                                                  op=ALU.mult),
                 reads=[bkey(bo), P.K("denD")], writes=[P.K("ytokD", y)])

        def d_post_pe(it):
            y = it["ti"] % 2
            ti = it["ti"]
            k = it["k"]
            for c in range(2):
                P.op("pe", lambda e, c=c: e.transpose(
                    out=tbD[:, c * 512 + ti * 128:c * 512 + (ti + 1) * 128], in_=ytokD[y][:, c * 128:(c + 1) * 128],
                    identity=ident[:, :]),
                    reads=[P.K("ytokD", y), "ident"], writes=[bkey(TB_BANK)])
            if it["last_of_sb"]:
                t0, n = it["t0"], it["n"]
                for c in range(2):
                    P.op("dve", lambda e, c=c: e.tensor_tensor(out=yT[:, 2 + c, t0:t0 + n], in0=tbD[:, c * 512:c * 512 + n],
                                                               in1=hT[:, 2 + c, t0:t0 + n], op=ALU.mult),
                         reads=[bkey(TB_BANK)] + hkeys(t0, n), writes=[("yT", 2 + c)])

        def d_warm(bo):
            P.op("pe", lambda e: e.matmul(gen[bo][:, 384:512], lhsT=ident[:, :], rhs=ident[:, :], start=True, stop=True),
                 writes=[bkey(bo)])

        run_pipeline(items, d_proj, d_post, d_post_pe, len(qsbs), d_warm, NWARM_D)
        gmode["wide"] = True

        nxt = layers[li + 1] if li + 1 < len(layers) else None
        if nxt is not None:
            nctx = norm_begin(nxt, True, nxs=8)
            pend_t, cur_n, pend_n = [], [], []

            def flush_group():
                if pend_n:
                    prev = pend_n.pop(0)
                    norm_batch(nctx, prev, "b")
                    norm_batch(nctx, prev, "c")

            def push_xs(t2):
                next_xs(nctx, t2)
                cur_n.append(t2)
                if len(cur_n) == 4:
                    flush_group()
                    pend_n.append(list(cur_n))
                    del cur_n[:]

            def after_tile_next(tl_):
                for t in tl_:
                    if t < NK[nxt]:
                        next_stats(nctx, t)
                        if pend_t:
                            push_xs(pend_t.pop(0))
                        pend_t.append(t)
            outproj(1, after_sb=after_tile_next, per_tile=True)
            while pend_t:
                push_xs(pend_t.pop(0))
            if cur_n:
                pend_n.append(list(cur_n))
            while pend_n:
                flush_group()
        elif do_final:
            nctx = norm_begin(None, True)
            pend_f = []

            def after_tile_final(tl_):
                for t in tl_:
                    if t < out_tiles:
                        final_stats(nctx, t)
                        if pend_f:
                            final_finish(nctx, pend_f.pop(0))
                        pend_f.append(t)
            outproj(1, after_sb=after_tile_final, per_tile=True)
            while pend_f:
                final_finish(nctx, pend_f.pop(0))
        else:
            phase_begin()
            outproj(1)

    for li_, l_ in enumerate(layers):
        do_layer(li_, l_)

    if not do_final:
        for i in range(out_tiles):
            P.op("sp", lambda e, i=i: e.dma_start(out=out_d[i * TK:(i + 1) * TK, :], in_=x_res[:, i, :]),
                 reads=[("x", i)], dma=True)
    P.emit(final_wait_ops=P.dma_ops)
    return nc, P


def _rope_tables(ty):
    t = np.arange(NT_MAX * TK)
    pos = (t if ty == 0 else 4095 - t).astype(np.float32)
    inv_freq = (np.float32(10000.0) ** (-np.arange(0, 64, 2, dtype=np.float32) / np.float32(64))).astype(np.float32)
    ang = (pos[:, None] * inv_freq[None, :]).astype(np.float32)
    cos = np.cos(ang).astype(np.float32)
    sin = np.sin(ang).astype(np.float32)
    p = np.arange(128)
    d = p % 64
    fi = d % 32
    sgn = np.where(d < 32, -1.0, 1.0).astype(np.float32)
    cosT = np.ascontiguousarray(cos[:, fi].T)
    sinT = np.ascontiguousarray((sin[:, fi] * sgn[None, :]).T)
    return cosT, sinT


def _na_index(ty):
    valid = np.zeros((128, NA_NCH, 128), dtype=bool)
    dr = np.zeros((128, NA_NCH, 128), dtype=np.int64)
    dc = np.zeros((128, NA_NCH, 128), dtype=np.int64)
    kk = np.arange(128)
    qq = np.arange(128)
    for cls in range(3):
        for j in range(NA_CLASS_CH[cls]):
            gi = NA_CLASS_OFF[cls] + j
            qrow = 2 * cls + qq // 64
            qcol = qq % 64
            krow = 2 * j + kk // 64
            kcol = kk % 64
            if ty == 1:
                qrow, qcol, krow, kcol = 63 - qrow, 63 - qcol, 63 - krow, 63 - kcol
            r0 = np.clip(qrow - 4, 0, 56)
            c0 = np.clip(qcol - 8, 0, 48)
            vr = (krow[:, None] >= r0[None, :]) & (krow[:, None] < r0[None, :] + 8)
            vc = (kcol[:, None] >= c0[None, :]) & (kcol[:, None] < c0[None, :] + 16)
            valid[:, gi, :] = vr & vc
            dr[:, gi, :] = np.clip(krow[:, None] - qrow[None, :] + 7, 0, 14)
            dc[:, gi, :] = np.clip(kcol[:, None] - qcol[None, :], -15, 15) + 15
    return valid, dr, dc


_NA_IDX = {}
_mm = np.arange(128)
_PSW = np.zeros((128, 128), np.float32)
_PSW[(_mm // 64) * 64 + (_mm % 64 + 32) % 64, _mm] = 1.0


def _prep_core_inputs(core, inputs, layers):
    seq, ty = core // 2, core % 2
    x = inputs["x"]
    if ty == 0:
        xc = x[seq, 0:NT_MAX * TK]
    else:
        xc = x[seq, ::-1][0:NT_MAX * TK]
    m = {"x": np.ascontiguousarray(xc, dtype=np.float32)}
    cosT, sinT = _rope_tables(ty)
    m["rope_cos"], m["rope_sin"] = cosT, sinT
    if ty not in _NA_IDX:
        _NA_IDX[ty] = _na_index(ty)
    valid, dr, dc = _NA_IDX[ty]
    m["fg"] = np.ascontiguousarray(inputs["final_norm_g"].reshape(1, D))
    m["psw"] = _PSW
    for l in layers:
        m["win%d" % l] = np.ascontiguousarray(inputs["w_in"][l][:, COLPERM])
        m["wout%d" % l] = np.ascontiguousarray(inputs["w_out"][l][ROWPERM, :])
        m["ng%d" % l] = np.ascontiguousarray(inputs["norm_g"][l].reshape(1, D))
        caw = inputs["conv_a_w"][l]
        cbw = inputs["conv_b_w"][l]
        if ty == 1:
            caw, cbw = caw[::-1], cbw[::-1]
        m["caw%d" % l] = np.ascontiguousarray(caw.T.reshape(2, 128, 31).transpose(1, 0, 2))
        m["cbw%d" % l] = np.ascontiguousarray(cbw.T.reshape(2, 128, 3).transpose(1, 0, 2))
        m["cab%d" % l] = np.ascontiguousarray(inputs["conv_a_b"][l].reshape(2, 128).T)
        m["lng%d" % l] = np.ascontiguousarray(inputs["ln_a_g"][l].reshape(2, 128).T)
        m["lnb%d" % l] = np.ascontiguousarray(inputs["ln_a_b"][l].reshape(2, 128).T)
        m["sink%d" % l] = np.ascontiguousarray(inputs["swa_sink"][l][SINK_ORDER].reshape(1, 4))
        rpb = inputs["na_rpb"][l]
        g = rpb[:, dr, dc]
        g = np.where(valid[None], g, np.float32(-30000.0))
        m["nab%d" % l] = np.ascontiguousarray(g.transpose(1, 2, 0, 3).astype(np.float32))
    return m


_PROGS = {}


def _get_prog(key, **kw):
    if key not in _PROGS:
        _PROGS[key] = build_program(**kw)[0]
    return _PROGS[key]


FUSED = True
PIPE_DEPTH = 2
SWA_MASK_ON_POOL = True
C_MASK_SPLIT = False
SSQ_ON_DVE = False
PE_DEFER = 2
NWARM_C = 0
NWARM_D = 0


def kernel(**inputs):
    inputs = {k: np.asarray(v) for k, v in inputs.items()}
    ncore = 8
    if FUSED:
        nc = _get_prog("fused", layers=(0, 1), do_final=True, x_in_tiles=NK[0], out_tiles=NOUT)
        maps = [_prep_core_inputs(c, inputs, (0, 1)) for c in range(ncore)]
        res = run_bass_kernel_spmd(nc, maps, core_ids=list(range(ncore)))
        outs = [r["out"] for r in res.results]
    else:
        nc1 = _get_prog("l0", layers=(0,), do_final=False, x_in_tiles=NK[0], out_tiles=NQ[0])
        maps = [_prep_core_inputs(c, inputs, (0,)) for c in range(ncore)]
        res1 = run_bass_kernel_spmd(nc1, maps, core_ids=list(range(ncore)))
        nc2 = _get_prog("l1", layers=(1,), do_final=True, x_in_tiles=NK[1], out_tiles=NOUT)
        maps2 = []
        for c in range(ncore):
            m = _prep_core_inputs(c, inputs, (1,))
            m["x"] = np.ascontiguousarray(res1.results[c]["out"])
            maps2.append(m)
        res2 = run_bass_kernel_spmd(nc2, maps2, core_ids=list(range(ncore)))
        outs = [r["out"] for r in res2.results]
    B, S = inputs["x"].shape[0], inputs["x"].shape[1]
    out = np.empty((B, S, D), dtype=np.float32)
    for c in range(ncore):
        seq, ty = c // 2, c % 2
        o = np.asarray(outs[c]).reshape(NOUT * TK, D)
        if ty == 0:
            out[seq, 0:NOUT * TK] = o
        else:
            out[seq, S - NOUT * TK:] = o[::-1]
    return out
```
